# Optimizing a Trainium2 kernel written in Bass

```python
import math
import jax, jax.numpy as jnp
from jax import lax
import numpy as np

D_MODEL = 1024
BATCH = 4
SEQ = 4096
DEPTH = 4

N_A_LAYERS = DEPTH // 2
N_B_LAYERS = DEPTH - N_A_LAYERS

EXPAND = 2
D_A = EXPAND * D_MODEL
POOL_WINDOWS = (2, 4, 8, 16)
N_POOL_GROUPS = len(POOL_WINDOWS)
POOL_GW = D_A // N_POOL_GROUPS

ATTN_PAIRS = ((128, 1), (512, 4), (2048, 16))
N_GROUPS = len(ATTN_PAIRS)
N_HEADS = 16
HEAD_DIM = D_MODEL // N_HEADS
D_B = N_HEADS * HEAD_DIM
BLK = 128

DN_ALPHA = (2 * DEPTH) ** 0.25
DN_BETA = (8 * DEPTH) ** -0.25
LN_EPS = 1e-5
NEG_INF = -1e30

kernel_name = "yoco_pool_dilated_attn_deepnorm"


def layer_norm(x, g, b):
    xf = x.astype(jnp.float32)
    mu = jnp.mean(xf, axis=-1, keepdims=True)
    var = jnp.mean(jnp.square(xf - mu), axis=-1, keepdims=True)
    y = (xf - mu) * lax.rsqrt(var + LN_EPS) * g.astype(jnp.float32) + b.astype(jnp.float32)
    return y.astype(x.dtype)


def alibi_slopes():
    n = N_GROUPS * N_HEADS
    i = jnp.arange(1, n + 1, dtype=jnp.float32)
    return (2.0 ** (-8.0 * i / n)).reshape(N_GROUPS, N_HEADS)


def pool_mixer(h, w_in, w_grp, scale, w_out):
    B, S, _ = h.shape
    proj = h @ w_in
    u, gate = proj[..., :D_A], proj[..., D_A:]
    uf = u.astype(jnp.float32).reshape(B, S, N_POOL_GROUPS, POOL_GW)
    pos = jnp.arange(S)
    pooled = []
    for g, w in enumerate(POOL_WINDOWS):
        ug = uf[:, :, g]
        cs = jnp.cumsum(ug, axis=1)
        shifted = jnp.pad(cs, ((0, 0), (w, 0), (0, 0)))[:, :S]
        cnt = jnp.minimum(pos + 1, w).astype(jnp.float32)
        pooled.append((cs - shifted) / cnt[None, :, None] - ug)
    pooled = jnp.stack(pooled, axis=2)
    mixed = jnp.einsum('bsgc,gce->bsge', pooled, w_grp.astype(jnp.float32))
    mixed = (mixed.reshape(B, S, D_A) * scale.astype(jnp.float32)).astype(h.dtype)
    return (mixed * jax.nn.silu(gate)) @ w_out


def dilated_window_attention(q, k, v, window, dilation, slopes):
    B, S, H, dh = q.shape
    L = S // dilation
    nblk = -(-L // BLK)
    Lp = nblk * BLK
    win_sub = window // dilation

    def to_sub(t):
        t = t.reshape(B, L, dilation, H, dh)
        t = jnp.pad(t, ((0, 0), (0, Lp - L), (0, 0), (0, 0), (0, 0)))
        return t.reshape(B, nblk, BLK, dilation, H, dh)

    qb, kb, vb = to_sub(q), to_sub(k), to_sub(v)

    def with_prev(t):
        prev = jnp.concatenate([jnp.zeros_like(t[:, :1]), t[:, :-1]], axis=1)
        return jnp.concatenate([prev, t], axis=2)

    kk, vv = with_prev(kb), with_prev(vb)
    scores = jnp.einsum('bnqrhd,bnkrhd->bnrhqk', qb, kk) * (HEAD_DIM ** -0.5)

    a = jnp.arange(BLK)[:, None]
    c = jnp.arange(2 * BLK)[None, :]
    rel = BLK + a - c
    key_pos = (jnp.arange(nblk)[:, None, None] - 1) * BLK + c[None]
    valid = (rel >= 0)[None] & (rel <= win_sub)[None] & (key_pos >= 0)
    bias = -slopes[:, None, None] * (dilation * rel).astype(jnp.float32)[None]
    scores = jnp.where(valid[None, :, None, None], scores + bias[None, None, None], NEG_INF)

    lse = jax.nn.logsumexp(scores, axis=-1)
    p = jnp.exp(scores - lse[..., None])
    out = jnp.einsum('bnrhqk,bnkrhd->bnqrhd', p, vv)
    out = out.reshape(B, Lp, dilation, H, dh)[:, :L].reshape(B, S, H, dh)
    lse = jnp.transpose(lse, (0, 1, 4, 2, 3)).reshape(B, Lp, dilation, H)[:, :L].reshape(B, S, H)
    return out, lse


def dilated_mixer(h, w_in, w_out, k_sh, v_sh, slopes):
    B, S, _ = h.shape
    proj = h @ w_in
    q_all = proj[..., :N_GROUPS * D_B].astype(jnp.float32).reshape(B, S, N_GROUPS, N_HEADS, HEAD_DIM)
    gate = proj[..., N_GROUPS * D_B:]
    kf = k_sh.astype(jnp.float32)
    vf = v_sh.astype(jnp.float32)
    outs, lses = [], []
    for g, (window, dilation) in enumerate(ATTN_PAIRS):
        o, l = dilated_window_attention(q_all[:, :, g], kf[:, :, g], vf[:, :, g], window, dilation, slopes[g])
        outs.append(o)
        lses.append(l)
    wts = jax.nn.softmax(jnp.stack(lses, axis=0), axis=0)
    merged = jnp.sum(wts[..., None] * jnp.stack(outs, axis=0), axis=0)
    merged = merged.reshape(B, S, D_B).astype(h.dtype)
    return (merged * jax.nn.silu(gate)) @ w_out


def setup_inputs(seed: int = 0) -> dict:
    key = jax.random.key(seed)
    ks = jax.random.split(key, 12)
    f32 = jnp.float32
    x = jax.random.normal(ks[0], (BATCH, SEQ, D_MODEL), f32)
    w_in_a = jax.random.normal(ks[1], (N_A_LAYERS, D_MODEL, 2 * D_A), f32) * D_MODEL ** -0.5
    w_grp_a = jax.random.normal(ks[2], (N_A_LAYERS, N_POOL_GROUPS, POOL_GW, POOL_GW), f32) * POOL_GW ** -0.5
    scale_a = 1.0 + 0.1 * jax.random.normal(ks[3], (N_A_LAYERS, D_A), f32)
    w_out_a = jax.random.normal(ks[4], (N_A_LAYERS, D_A, D_MODEL), f32) * (D_A ** -0.5) * DN_BETA
    w_kv = jax.random.normal(ks[5], (D_MODEL, 2 * N_GROUPS * D_B), f32) * D_MODEL ** -0.5
    w_in_b = jax.random.normal(ks[6], (N_B_LAYERS, D_MODEL, (N_GROUPS + 1) * D_B), f32) * D_MODEL ** -0.5
    w_out_b = jax.random.normal(ks[7], (N_B_LAYERS, D_B, D_MODEL), f32) * (D_B ** -0.5) * DN_BETA
    ln_g = 1.0 + 0.02 * jax.random.normal(ks[8], (DEPTH, D_MODEL), f32)
    ln_b = 0.02 * jax.random.normal(ks[9], (DEPTH, D_MODEL), f32)
    return {"x": x, "w_in_a": w_in_a, "w_grp_a": w_grp_a, "scale_a": scale_a, "w_out_a": w_out_a,
            "w_kv": w_kv, "w_in_b": w_in_b, "w_out_b": w_out_b, "ln_g": ln_g, "ln_b": ln_b}


def reference(x, w_in_a, w_grp_a, scale_a, w_out_a, w_kv, w_in_b, w_out_b, ln_g, ln_b):
    B, S, _ = x.shape
    slopes = alibi_slopes()
    h = x
    k_sh = None
    v_sh = None
    for l in range(DEPTH):
        if l < N_A_LAYERS:
            y = pool_mixer(h, w_in_a[l], w_grp_a[l], scale_a[l], w_out_a[l])
        else:
            j = l - N_A_LAYERS
            y = dilated_mixer(h, w_in_b[j], w_out_b[j], k_sh, v_sh, slopes)
        h = layer_norm(DN_ALPHA * h + y, ln_g[l], ln_b[l])
        if l == N_A_LAYERS - 1:
            kv = h @ w_kv
            k_sh = kv[..., :N_GROUPS * D_B].reshape(B, S, N_GROUPS, N_HEADS, HEAD_DIM)
            v_sh = kv[..., N_GROUPS * D_B:].reshape(B, S, N_GROUPS, N_HEADS, HEAD_DIM)
    return h
```

```python
import contextlib
import numpy as np
import ml_dtypes
import concourse.bass as bass
import concourse.mybir as mybir
from concourse.bass_utils import run_bass_kernel_spmd

F32 = mybir.dt.float32
BF16 = mybir.dt.bfloat16
AF = mybir.ActivationFunctionType
ALU = mybir.AluOpType

ENGS = ("pe", "act", "dve", "pool", "sp")


class Res:
    __slots__ = ("name", "writer", "readers")

    def __init__(self, name=""):
        self.name = name
        self.writer = None
        self.readers = []


class Op:
    __slots__ = ("eng", "fn", "deps", "marked", "idx", "is_dma", "key", "kidx", "inc")

    def __init__(self, eng, fn, is_dma=False, key=None, inc=16):
        self.inc = inc
        self.eng = eng
        self.fn = fn
        self.deps = []
        self.marked = False
        self.idx = None
        self.is_dma = is_dma
        self.key = key
        self.kidx = None


class Prog:
    def __init__(self, nc):
        self.nc = nc
        self.ops = {e: [] for e in ENGS}
        self.key_count = {}
        self.final_waits = []

    def _dep(self, op, prod):
        if prod is None or prod is op:
            return
        if (not prod.is_dma) and (not op.is_dma) and prod.eng == "pe" and op.eng == "pe":
            return
        prod.marked = True
        op.deps.append(prod)

    def barrier(self, exclude=()):
        lasts = []
        for e in ENGS:
            comp = [o for o in self.ops[e] if not o.is_dma and o.fn is not None]
            if comp:
                lasts.append(comp[-1])
        lastk = {}
        for e in ENGS:
            for o in self.ops[e]:
                if o.is_dma and o.key not in exclude:
                    lastk[o.key] = o
        lasts += list(lastk.values())
        for o in lasts:
            o.marked = True
        for e in ENGS:
            op = Op(e, None)
            op.deps = list(lasts)
            self.ops[e].append(op)

    def add(self, eng, fn, reads=(), writes=(), is_dma=False, key=None, inc=16):
        op = Op(eng, fn, is_dma, key, inc)
        if is_dma:
            self.key_count[key] = self.key_count.get(key, 0) + 1
            op.kidx = self.key_count[key]
        for r in reads:
            self._dep(op, r.writer)
        for w in writes:
            self._dep(op, w.writer)
            for rd in w.readers:
                self._dep(op, rd)
        for r in reads:
            r.readers.append(op)
        for w in writes:
            w.writer = op
            w.readers = []
        self.ops[eng].append(op)
        return op

    def dma(self, eng, out, in_, reads=(), writes=(), key=None, **kw):
        def fn(e, out=out, in_=in_, kw=kw):
            return e.dma_start(out=out, in_=in_, **kw)
        return self.add(eng, fn, reads, writes, is_dma=True, key=key)

    def finish(self, eng, ops):
        for o in ops:
            o.marked = True
        self.final_waits.append((eng, list(ops)))

    def emit(self):
        nc = self.nc
        for e in ENGS:
            c = 0
            for op in self.ops[e]:
                if op.is_dma:
                    continue
                if op.marked:
                    c += 1
                    op.idx = c
        with contextlib.ExitStack() as st:
            esem = {e: st.enter_context(nc.semaphore("s_" + e)) for e in ENGS}
            ksem = {k: st.enter_context(nc.semaphore("k_" + str(k))) for k in self.key_count}
            block = st.enter_context(nc.Block())

            def semval(prod):
                if prod.is_dma:
                    return ksem[prod.key], prod.inc * prod.kidx
                return esem[prod.eng], prod.idx

            def do_waits(eng, waited, prods):
                need = {}
                for p in prods:
                    s, v = semval(p)
                    if need.get(s, 0) < v:
                        need[s] = v
                for s, v in need.items():
                    if waited.get(s, 0) < v:
                        eng.wait_ge(s, v)
                        waited[s] = v

            def run(engname, eng):
                waited = {}
                for op in self.ops[engname]:
                    do_waits(eng, waited, op.deps)
                    if op.fn is None:
                        continue
                    ins = op.fn(eng)
                    if op.is_dma:
                        ins.then_inc(ksem[op.key], op.inc)
                    elif op.marked:
                        ins.then_inc(esem[engname], 1)
                for (fe, fops) in self.final_waits:
                    if fe == engname:
                        do_waits(eng, waited, fops)

            @block.tensor
            def _(eng):
                run("pe", eng)

            @block.scalar
            def _(eng):
                run("act", eng)

            @block.vector
            def _(eng):
                run("dve", eng)

            @block.gpsimd
            def _(eng):
                run("pool", eng)

            @block.sync
            def _(eng):
                run("sp", eng)


D = 1024
T = 2048
TH = 32
TX = T + TH
NTT = T // 128
DA = 2048
POOL_W = (2, 4, 8, 16)
ALPHA = float(8.0 ** 0.25)
EPS = 1e-5
DIL = (1, 4, 16)
NWS = 6
NPS = 7
NEG = -30000.0


class Ctx:
    pass


def _mk_common(nc, P, st):
    c = Ctx()
    c.nc, c.P, c.st = nc, P, st

    c.cur = st

    def sb(name, shape, dt):
        return c.cur.enter_context(nc.sbuf_tensor("sb_" + name, shape, dt))

    def ps(name, shape, dt):
        return st.enter_context(nc.psum_tensor("pp_" + name, shape, dt))

    c.sb, c.ps = sb, ps
    c.PS = [ps("ps%d" % i, [128, 512], F32) for i in range(NPS)]
    c.RPS = [Res("ps%d" % i) for i in range(NPS)]
    c.psi = 0
    c.PT = ps("pst", [128, 8, 128], BF16)
    c.RPT = Res("pst")
    c.WR = [sb("wr%d" % i, [128, 2048], BF16) for i in range(NWS)]
    c.RWR = [Res("wr%d" % i) for i in range(NWS)]
    c.wi = 0
    c.hT = sb("hT", [128, 8, TX], BF16)
    c.RhT = [Res("hT%d" % i) for i in range(NTT + 1)]
    return c


def psum_next(c):
    i = c.psi % NPS
    c.psi += 1
    return c.PS[i], c.RPS[i]


def wload(c, src_ap, view):
    i = c.wi % NWS
    c.wi += 1
    t = view(c.WR[i][:])
    ws = [c.RWR[i]] + ([c.RWR2[i]] if hasattr(c, "RWR2") else [])
    c.P.dma("pool", t, src_ap, writes=ws, key="w%d" % i)
    return t, c.RWR[i]


def wload2(c, srcs):
    i = c.wi % NWS
    c.wi += 1
    t = c.WR[i][:].rearrange("p (k g c) -> p k g c", k=8, g=2)
    if not hasattr(c, "RWR2"):
        c.RWR2 = [Res("wrb%d" % q) for q in range(NWS)]
    rs = [c.RWR[i], c.RWR2[i]]
    for gi, src in enumerate(srcs):
        c.P.dma("pool", t[:, :, gi, :], src, writes=[rs[gi]], key=("w%d" % i) if gi == 0 else ("w%db" % i))
    return t, rs


def emit_ln_tile(c, Htile, rH, npart, Gb, Bb, rGB, hb, rhb, sm, rsm, eps_t):
    P = c.P
    stats, mv, rstd, nmr = sm
    P.add("dve", lambda e: e.bn_stats(out=stats[0:npart, 0, :], in_=Htile[0:npart, 0:512]), reads=[rH], writes=[rsm])
    P.add("dve", lambda e: e.bn_stats(out=stats[0:npart, 1, :], in_=Htile[0:npart, 512:1024]), reads=[rH], writes=[rsm])
    P.add("dve", lambda e: e.bn_aggr(out=mv[0:npart, :], in_=stats[0:npart, :, :]), reads=[rsm], writes=[rsm])
    P.add("dve", lambda e: e.tensor_scalar(out=rstd[0:npart, :], in0=mv[0:npart, 1:2], scalar1=EPS, scalar2=None, op0=ALU.add), reads=[rsm], writes=[rsm])
    P.add("act", lambda e: e.sqrt(out=rstd[0:npart, :], in_=rstd[0:npart, :]), reads=[rsm], writes=[rsm])
    P.add("dve", lambda e: e.reciprocal(out=rstd[0:npart, :], in_=rstd[0:npart, :]), reads=[rsm], writes=[rsm])
    P.add("dve", lambda e: e.scalar_tensor_tensor(out=nmr[0:npart, :], in0=mv[0:npart, 0:1], scalar=-1.0, in1=rstd[0:npart, :], op0=ALU.mult, op1=ALU.mult), reads=[rsm], writes=[rsm])
    P.add("act", lambda e: e.activation(out=Htile[0:npart, :], in_=Htile[0:npart, :], func=AF.Identity, bias=nmr[0:npart, :], scale=rstd[0:npart, :]), reads=[rH, rsm], writes=[rH])
    P.add("dve", lambda e: e.tensor_tensor(out=Htile[0:npart, :], in0=Htile[0:npart, :], in1=Gb[0:npart, :], op=ALU.mult), reads=[rH, rGB], writes=[rH])
    P.add("dve", lambda e: e.tensor_tensor(out=Htile[0:npart, :], in0=Htile[0:npart, :], in1=Bb[0:npart, :], op=ALU.add), reads=[rH, rGB], writes=[rH])
    P.add("act", lambda e: e.copy(out=hb[0:npart, :], in_=Htile[0:npart, :]), reads=[rH], writes=[rhb])


def emit_transpose_tile(c, hb, rhb, npart, hT, col0, rhT, ident, rid):
    P = c.P
    PT, RPT = c.PT, c.RPT
    for k in range(8):
        P.add("pe", lambda e, k=k: e.transpose(out=PT[:, k, 0:npart], in_=hb[0:npart, k * 128:(k + 1) * 128], identity=ident[0:npart, 0:npart]),
              reads=[rhb, rid], writes=[RPT])
    P.add("dve", lambda e: e.tensor_copy(out=hT[:, :, col0:col0 + npart], in_=PT[:, :, 0:npart]), reads=[RPT], writes=[rhT])


class LNPipe:
    NSM = 4
    NHB = 3

    def __init__(self, c, prefix, Gb, Bb, rGB, hT, ident, rid, do_transpose=True):
        self.c, self.Gb, self.Bb, self.rGB = c, Gb, Bb, rGB
        self.hT, self.ident, self.rid = hT, ident, rid
        self.do_transpose = do_transpose
        sb = c.sb
        self.sm = [(sb(prefix + "st%d" % i, [128, 2, 6], F32), sb(prefix + "mv%d" % i, [128, 2], F32), sb(prefix + "rs%d" % i, [128, 1], F32), sb(prefix + "nm%d" % i, [128, 1], F32)) for i in range(self.NSM)]
        self.rsm = [Res(prefix + "sm%d" % i) for i in range(self.NSM)]
        self.hb = [sb(prefix + "hb%d" % i, [128, 1024], BF16) for i in range(self.NHB)]
        self.rhb = [Res(prefix + "hb%d" % i) for i in range(self.NHB)]
        self.n = 0
        self.q = []

    def _s1(self, x):
        P = self.c.P
        Ht, rH, npart = x["H"], x["rH"], x["np"]
        stats, mv, rstd, nmr = self.sm[x["si"]]
        rsm = self.rsm[x["si"]]
        P.add("dve", lambda e: e.bn_stats(out=stats[0:npart, 0, :], in_=Ht[0:npart, 0:512]), reads=[rH], writes=[rsm])
        P.add("dve", lambda e: e.bn_stats(out=stats[0:npart, 1, :], in_=Ht[0:npart, 512:1024]), reads=[rH], writes=[rsm])
        P.add("dve", lambda e: e.bn_aggr(out=mv[0:npart, :], in_=stats[0:npart, :, :]), reads=[rsm], writes=[rsm])
        P.add("dve", lambda e: e.tensor_scalar(out=rstd[0:npart, :], in0=mv[0:npart, 1:2], scalar1=EPS, scalar2=None, op0=ALU.add), reads=[rsm], writes=[rsm])
        P.add("act", lambda e: e.sqrt(out=rstd[0:npart, :], in_=rstd[0:npart, :]), reads=[rsm], writes=[rsm])

    def _s2(self, x):
        P = self.c.P
        Ht, rH, npart = x["H"], x["rH"], x["np"]
        stats, mv, rstd, nmr = self.sm[x["si"]]
        rsm = self.rsm[x["si"]]
        P.add("dve", lambda e: e.reciprocal(out=rstd[0:npart, :], in_=rstd[0:npart, :]), reads=[rsm], writes=[rsm])
        P.add("dve", lambda e: e.scalar_tensor_tensor(out=nmr[0:npart, :], in0=mv[0:npart, 0:1], scalar=-1.0, in1=rstd[0:npart, :], op0=ALU.mult, op1=ALU.mult), reads=[rsm], writes=[rsm])
        P.add("act", lambda e: e.activation(out=Ht[0:npart, :], in_=Ht[0:npart, :], func=AF.Identity, bias=nmr[0:npart, :], scale=rstd[0:npart, :]), reads=[rH, rsm], writes=[rH])

    def _s3(self, x):
        P = self.c.P
        Ht, rH, npart = x["H"], x["rH"], x["np"]
        Gb, Bb, rGB = self.Gb, self.Bb, self.rGB
        hb, rhb = self.hb[x["hi"]], self.rhb[x["hi"]]
        P.add("pool", lambda e: e.tensor_tensor(out=Ht[0:npart, :], in0=Ht[0:npart, :], in1=Gb[0:npart, :], op=ALU.mult), reads=[rH, rGB], writes=[rH])
        P.add("pool", lambda e: e.tensor_tensor(out=Ht[0:npart, :], in0=Ht[0:npart, :], in1=Bb[0:npart, :], op=ALU.add), reads=[rH, rGB], writes=[rH])
        if self.do_transpose and x.get("tr", True):
            P.add("act", lambda e: e.copy(out=hb[0:npart, :], in_=Ht[0:npart, :]), reads=[rH], writes=[rhb])
        if x.get("after") is not None:
            x["after"]()

    def _s4(self, x):
        if self.do_transpose and x.get("tr", True):
            emit_transpose_tile(self.c, self.hb[x["hi"]], self.rhb[x["hi"]], x["np"], self.hT, x["col0"], x["rhT"], self.ident, self.rid)

    def _advance(self):
        stages = [self._s1, self._s2, self._s3, self._s4]
        for item in self.q:
            stages[item[0]](item[1])
            item[0] += 1
        self.q = [it for it in self.q if it[0] < 4]

    def push(self, Htile, rH, npart, col0=None, rhT=None, after=None, tr=True):
        x = {"H": Htile, "rH": rH, "np": npart, "col0": col0, "rhT": rhT, "after": after, "tr": tr,
             "si": self.n % self.NSM, "hi": self.n % self.NHB}
        self.n += 1
        self.q.append([0, x])
        self._advance()

    def flush(self):
        while self.q:
            self._advance()


def build_phase_a(nc, c, io):
    P = c.P
    sb = c.sb
    H = sb("H", [128, NTT + 1, 1024], F32)
    hT = c.hT
    U = [sb("U%d" % i, [128, 16 + TX], F32) for i in range(2)]
    A_ = sb("poolA", [128, 560], F32)
    B_ = sb("poolB", [128, 560], F32)
    PTb = sb("pooledT", [128, 4, TX], BF16)
    SG = [sb("SG%d" % i, [128, TX], BF16) for i in range(3)]
    GT = sb("GT", [128, 4, TX], BF16)
    Gb = sb("Gb", [128, 1024], F32)
    Bb = sb("Bb", [128, 1024], F32)
    scl = sb("scl", [128, 2, 16], F32)
    cst = sb("cst", [128, 80], F32)
    ident = sb("ident", [128, 128], BF16)
    tmp16 = sb("tmp16", [128, 16], F32)

    RH = [Res("H%d" % i) for i in range(NTT + 1)]
    RhT = c.RhT
    RU = [Res("U0"), Res("U1")]
    RA, RB = Res("A"), Res("B")
    RPTb = [Res("pooledT%d" % i) for i in range(4)]
    RSG = [Res("SG0"), Res("SG1"), Res("SG2")]
    RGT = [Res("GT%d" % i) for i in range(4)]
    RGB = Res("GB")
    Rscl, Rcst, Rid, Rt16 = Res("scl"), Res("cst"), Res("ident"), Res("t16")
    Rzero = Res("zero")

    hflag = cst[:, 64:65]
    eps_t = cst[:, 65:66]

    P.dma("sp", cst[:], io["cst"], writes=[Rcst], key="c0")
    P.dma("pool", ident[:], io["ident"], writes=[Rid], key="c1")
    P.dma("sp", scl[:], io["scale_a"].rearrange("l (c p) -> p l c", p=128), writes=[Rscl], key="c2", allow_slow_non_contiguous=True)
    P.dma("sp", H[0:TH, 0, :], io["xh"][0:TH, :], writes=[RH[0]], key="x0")
    for i in range(NTT):
        P.dma("sp", H[:, 1 + i, :], io["xh"][TH + 128 * i: TH + 128 * (i + 1), :], writes=[RH[1 + i]], key="x%d" % (1 + i))
    for b in range(2):
        P.add("dve", lambda e, b=b: e.memset(U[b][:, 0:16], 0.0), writes=[RU[b]])

    tiles = [(0, TH, 0)] + [(1 + i, 128, TH + 128 * i) for i in range(NTT)]
    segs = [(0, TH, [0])] + [(TH + 512 * s, 512, [1 + 4 * s + q for q in range(4)]) for s in range(T // 512)]

    lnp = LNPipe(c, "lnA", Gb, Bb, RGB, hT, ident, Rid)
    for (ti, rows, col0) in tiles:
        b = ti % lnp.NHB
        P.add("act", lambda e, ti=ti, rows=rows, b=b: e.copy(out=lnp.hb[b][0:rows, :], in_=H[0:rows, ti, :]), reads=[RH[ti]], writes=[lnp.rhb[b]])
        emit_transpose_tile(c, lnp.hb[b], lnp.rhb[b], rows, hT, col0, RhT[ti], ident, Rid)

    def linear_fm(wt, rwt, ocol, kch, rhs_fn, rhs_res_fn, seglist, evac):
        for (c0, n, tids) in seglist:
            pst, rps = psum_next(c)
            for k in range(kch):
                P.add("pe", lambda e, k=k, c0=c0, n=n, pst=pst: e.matmul(pst[:, 0:n], lhsT=wt[:, k, ocol:ocol + 128], rhs=rhs_fn(k, c0, n), start=(k == 0), stop=(k == kch - 1)),
                      reads=[rwt] + rhs_res_fn(c0, n, tids), writes=[rps])
            evac(pst, rps, c0, n, tids)

    for l in range(2):
        last = (l == 1)
        P.dma("sp", Gb[:], io["ln_g"][l:l + 1, :].partition_broadcast(128).rearrange("p o d -> p (o d)"), writes=[RGB], key="gb")
        P.dma("sp", Bb[:], io["ln_b"][l:l + 1, :].partition_broadcast(128).rearrange("p o d -> p (o d)"), writes=[RGB], key="gb")
        w_in = io["w_in_a"][l].rearrange("(k p) c -> p k c", p=128)
        segs_own = segs[1:]
        segs_g = segs if not last else segs_own
        for g in range(4):
            wdw = POOL_W[g]
            for oc in range(4):
                fc = 4 * g + oc
                if oc % 2 == 0:
                    wt, rwt = wload(c, w_in[:, :, fc * 128: fc * 128 + 256], lambda t: t.rearrange("p (k c) -> p k c", k=8))
                ub = U[fc % 2]
                rub = RU[fc % 2]

                def evac_u(pst, rps, c0, n, tids, ub=ub, rub=rub):
                    if c0 == 0:
                        P.add("act", lambda e: e.activation(out=ub[:, 16:16 + n], in_=pst[:, 0:n], func=AF.Copy, scale=hflag), reads=[rps, Rcst], writes=[rub])
                    else:
                        P.add("act", lambda e: e.copy(out=ub[:, 16 + c0:16 + c0 + n], in_=pst[:, 0:n]), reads=[rps], writes=[rub])

                linear_fm(wt, rwt, (oc % 2) * 128, 8, lambda k, c0, n: hT[:, k, c0:c0 + n], lambda c0, n, tids: [RhT[t] for t in tids], segs, evac_u)
                pool_segs = [(16, TH + 512)] + [(16 + TH + 512 * s, 512) for s in range(1, T // 512)] if not last else [(16 + TH + 512 * s, 512) for s in range(T // 512)]
                for (p0, n) in pool_segs:
                    W_ = n + 16
                    us = ub[:, p0 - 16:p0 + n]
                    stages = []
                    srcs = [(us, rub)]
                    bufs = [(A_, RA), (B_, RB)]
                    sh = 1
                    nst = g + 1
                    for si in range(nst):
                        (src, rsrc) = srcs[-1]
                        (dst, rdst) = bufs[si % 2]
                        lo = 2 * sh - 1
                        P.add("dve", lambda e, src=src, dst=dst, lo=lo, sh=sh, W_=W_: e.tensor_tensor(out=dst[:, lo:W_], in0=src[:, lo:W_], in1=src[:, lo - sh:W_ - sh], op=ALU.add),
                              reads=[rsrc], writes=[rdst])
                        srcs.append((dst[:, 0:W_], rdst))
                        sh *= 2
                    (S_, rS) = srcs[-1]
                    o0 = p0 - 16
                    P.add("dve", lambda e, S_=S_, us=us, W_=W_, o0=o0, n=n, oc=oc, wdw=wdw: e.scalar_tensor_tensor(out=PTb[:, oc, o0:o0 + n], in0=S_[:, 16:W_], scalar=1.0 / wdw, in1=us[:, 16:W_], op0=ALU.mult, op1=ALU.subtract),
                          reads=[rS, rub], writes=[RPTb[oc]])
                    if p0 <= 16 + TH < p0 + n:
                        q = 16 + TH - (p0 - 16)
                        P.add("dve", lambda e, S_=S_, q=q, g=g: e.tensor_tensor(out=tmp16[:], in0=S_[:, q:q + 16], in1=cst[:, 16 * g:16 * g + 16], op=ALU.mult), reads=[rS, Rcst], writes=[Rt16])
                        P.add("dve", lambda e, us=us, q=q, oc=oc: e.tensor_tensor(out=PTb[:, oc, TH:TH + 16], in0=tmp16[:], in1=us[:, q:q + 16], op=ALU.subtract), reads=[Rt16, rub], writes=[RPTb[oc]])
            gstate = {}

            def do_gate(oc, g=g, l=l):
                fc = 4 * g + oc
                if oc % 2 == 0:
                    gstate["wtg"] = wload(c, w_in[:, :, DA + fc * 128: DA + fc * 128 + 256], lambda t: t.rearrange("p (k c) -> p k c", k=8))
                wtg, rwtg = gstate["wtg"]
                sgb, rsgb = SG[oc % 3], RSG[oc % 3]

                def evac_gate(pst, rps, c0, n, tids, sgb=sgb, rsgb=rsgb):
                    P.add("act", lambda e: e.activation(out=sgb[:, c0:c0 + n], in_=pst[:, 0:n], func=AF.Silu), reads=[rps], writes=[rsgb])

                linear_fm(wtg, rwtg, (oc % 2) * 128, 8, lambda k, c0, n: hT[:, k, c0:c0 + n], lambda c0, n, tids: [RhT[t] for t in tids], segs_g, evac_gate)

            def do_grp(oc, g=g, l=l):
                fc = 4 * g + oc
                if "wgrp" not in gstate:
                    gstate["wgrp"] = wload(c, io["w_grp_a"][l, g].rearrange("(k p) c -> p k c", p=128), lambda t: t.rearrange("p (k c) -> p k c", k=4))
                wgrp, rwgrp = gstate["wgrp"]
                sgb, rsgb = SG[oc % 3], RSG[oc % 3]

                def evac_mix(pst, rps, c0, n, tids, sgb=sgb, rsgb=rsgb, oc=oc, fc=fc, l=l):
                    P.add("dve", lambda e: e.scalar_tensor_tensor(out=GT[:, oc, c0:c0 + n], in0=pst[:, 0:n], scalar=scl[:, l, fc:fc + 1], in1=sgb[:, c0:c0 + n], op0=ALU.mult, op1=ALU.mult),
                          reads=[rps, rsgb, Rscl], writes=[RGT[oc]])

                linear_fm(wgrp, rwgrp, oc * 128, 4, lambda k, c0, n: PTb[:, k, c0:c0 + n], lambda c0, n, tids: list(RPTb), segs_g, evac_mix)

            do_gate(0)
            do_gate(1)
            do_grp(0)
            do_gate(2)
            do_grp(1)
            do_gate(3)
            do_grp(2)
            do_grp(3)
            w_out_g = io["w_out_a"][l, 512 * g:512 * (g + 1), :].rearrange("(k p) c -> p k c", p=128)
            tl = tiles if not last else tiles[1:]
            for hf in range(2):
                wo, rwo = wload(c, w_out_g[:, :, 512 * hf:512 * (hf + 1)], lambda t: t.rearrange("p (k c) -> p k c", k=4))
                for (ti, rows, col0) in tl:
                    pst, rps = psum_next(c)
                    for k in range(4):
                        P.add("pe", lambda e, k=k, pst=pst, rows=rows, col0=col0, wo=wo: e.matmul(pst[0:rows, :], lhsT=GT[:, k, col0:col0 + rows], rhs=wo[:, k, :], start=(k == 0), stop=(k == 3)),
                              reads=[rwo] + RGT, writes=[rps])
                    hs = H[0:rows, ti, 512 * hf:512 * (hf + 1)]
                    if g == 0:
                        P.add("dve", lambda e, hs=hs, pst=pst, rows=rows: e.scalar_tensor_tensor(out=hs, in0=hs, scalar=ALPHA, in1=pst[0:rows, :], op0=ALU.mult, op1=ALU.add), reads=[rps, RH[ti]], writes=[RH[ti]])
                    else:
                        P.add("dve", lambda e, hs=hs, pst=pst, rows=rows: e.tensor_tensor(out=hs, in0=hs, in1=pst[0:rows, :], op=ALU.add), reads=[rps, RH[ti]], writes=[RH[ti]])
                    if g == 3 and hf == 1:
                        lnp.push(H[:, ti, :], RH[ti], rows, col0=col0, rhT=RhT[ti])
            if g == 3:
                lnp.flush()

    w_kv = io["w_kv"].rearrange("(k p) c -> p k c", p=128)
    outs = []
    segs_own = segs[1:]
    tail = io.get("tail")
    pend_cc = []
    for g in (2, 1, 0):
        d = DIL[g]
        nb = T // (128 * d)
        rko, rvo = [], []
        for cc in range(8):
            if cc % 2 == 0:
                wt, rwt = wload(c, w_kv[:, :, g * 1024 + cc * 128: g * 1024 + cc * 128 + 256], lambda t: t.rearrange("p (k c) -> p k c", k=8))
            stg = PTb[:, cc % 2, 0:T]
            rstg = RPTb[cc % 2]

            def evac_k(pst, rps, c0, n, tids, stg=stg, rstg=rstg, d=d, nb=nb):
                s = (c0 - TH) // 512
                if d == 1:
                    P.add("act", lambda e: e.copy(out=stg[:, 512 * s:512 * (s + 1)], in_=pst[:, :]), reads=[rps], writes=[rstg])
                elif d == 4:
                    o = stg.rearrange("p (r n i) -> p r n i", r=4, n=nb)[:, :, s, :]
                    P.add("act", lambda e: e.copy(out=o, in_=pst[:, :].rearrange("p (i r) -> p r i", r=4)), reads=[rps], writes=[rstg])
                else:
                    o = stg.rearrange("p (r i) -> p r i", r=16)[:, :, 32 * s:32 * (s + 1)]
                    P.add("act", lambda e: e.copy(out=o, in_=pst[:, :].rearrange("p (i r) -> p r i", r=16)), reads=[rps], writes=[rstg])

            linear_fm(wt, rwt, (cc % 2) * 128, 8, lambda k, c0, n: hT[:, k, c0:c0 + n], lambda c0, n, tids: [RhT[t] for t in tids], segs_own, evac_k)
            rk = Res("ktd")
            rko.append(rk)
            outs.append(P.dma("sp", io["KT"][g, cc * 128:(cc + 1) * 128, :], stg, reads=[rstg], writes=[rk], key="kt%d" % (cc % 2)))
            if tail is not None:
                frows = slice(cc * 128, (cc + 1) * 128)
                parts = [(0, d)] if g < 2 else [(0, 8), (8, 16)]
                for (ra, rb) in parts:
                    cv, ck, off = tail_loc(g, ra)
                    n_ = 128 * (rb - ra)
                    r2 = Res("tlk")
                    src = stg.rearrange("p (r n i) -> p r n i", r=d, n=nb)[:, ra:rb, nb - 1, :]
                    P.dma("sp", tail["TL"][ck][frows, off:off + n_].rearrange("p (r i) -> p r i", i=128), src, reads=[rstg], writes=[r2], key="tk%d_%d" % (ck, cc % 2))
                    tail["rtl"][ck].append(r2)
        for vc in range(4):
            wt, rwt = wload(c, w_kv[:, :, 3072 + g * 1024 + vc * 256: 3072 + g * 1024 + (vc + 1) * 256], lambda t: t.rearrange("p (k c) -> p k c", k=8))
            vi = (g * 4 + vc) % 2
            stg = U[vi][:].bitcast(BF16)[:, 0:4096].rearrange("p (t c) -> p t c", t=NTT)
            rstg = RU[vi]
            for (ti, rows, col0) in tiles[1:]:
                pst, rps = psum_next(c)
                for k in range(8):
                    P.add("pe", lambda e, k=k, pst=pst, col0=col0, wt=wt: e.matmul(pst[:, 0:256], lhsT=hT[:, k, col0:col0 + 128], rhs=wt[:, k, :], start=(k == 0), stop=(k == 7)),
                          reads=[rwt, RhT[ti]], writes=[rps])
                P.add("dve", lambda e, pst=pst, ti=ti, stg=stg: e.tensor_copy(out=stg[:, ti - 1, :], in_=pst[:, 0:256]), reads=[rps], writes=[rstg])
            rv = Res("vd")
            rvo.append(rv)
            outs.append(P.dma("sp", io["V"][g].rearrange("(t p) c -> p t c", p=128)[:, :, vc * 256:(vc + 1) * 256], stg, reads=[rstg], writes=[rv], key="v%d" % vi))
        if tail is not None:
            TL, tl32, ga32, rtl, rga = tail["TL"], tail["tl32"], tail["ga32"], tail["rtl"], tail["rga"]
            vsrc = io["V"][g, T - 128 * d:T, :].rearrange("(i r) f -> r i f", r=d)
            ksrc = io["KT"][g].rearrange("p (r n i) -> p r n i", r=d, n=nb)[:, :, nb - 1, :]
            parts = [(0, d)] if g < 2 else [(0, 8), (8, 16)]
            for (ra, rb) in parts:
                cv, ck, off = tail_loc(g, ra)
                n_ = 128 * (rb - ra)
                r1 = Res("tlv")
                P.dma("sp", TL[cv][off:off + n_, :].rearrange("(r i) f -> r i f", i=128), vsrc[ra:rb], reads=rvo, writes=[r1], key="tlv%d%d" % (g, ra))
                rtl[cv].append(r1)
            for f in pend_cc:
                f()
            pend_cc = []
            chunks = {2: (4, 1, 5, 2), 1: (3, 0), 0: (7, 6)}[g]
            for ci in chunks:
                def issue(ci=ci):
                    P.add("pool", lambda e: e.collective_compute("AllGather", ALU.bypass, replica_groups=[[0, 1], [2, 3], [4, 5], [6, 7]], ins=[tl32[ci]], outs=[ga32[ci]]),
                          reads=rtl[ci], writes=[rga[ci]], is_dma=True, key="cc%d" % ci, inc=1)
                pend_cc.append(issue)
    for f in pend_cc:
        f()
    if "H1" in io:
        for i in range(NTT):
            outs.append(P.dma("sp", io["H1"][128 * i:128 * (i + 1), :], H[:, 1 + i, :], reads=[RH[1 + i]], key="ho"))
    return outs


def make_nc_a():
    nc = bass.Bass("TRN2", target_bir_lowering=False)
    io = {}
    io["xh"] = nc.dram_tensor("xh", [TX, D], F32, kind="ExternalInput").ap()
    io["w_in_a"] = nc.dram_tensor("w_in_a", [2, D, 2 * DA], F32, kind="ExternalInput").ap()
    io["w_grp_a"] = nc.dram_tensor("w_grp_a", [2, 4, 512, 512], F32, kind="ExternalInput").ap()
    io["scale_a"] = nc.dram_tensor("scale_a", [2, DA], F32, kind="ExternalInput").ap()
    io["w_out_a"] = nc.dram_tensor("w_out_a", [2, DA, D], F32, kind="ExternalInput").ap()
    io["w_kv"] = nc.dram_tensor("w_kv", [D, 6144], F32, kind="ExternalInput").ap()
    io["ln_g"] = nc.dram_tensor("ln_g", [4, D], F32, kind="ExternalInput").ap()
    io["ln_b"] = nc.dram_tensor("ln_b", [4, D], F32, kind="ExternalInput").ap()
    io["cst"] = nc.dram_tensor("cst", [128, 80], F32, kind="ExternalInput").ap()
    io["ident"] = nc.dram_tensor("ident", [128, 128], F32, kind="ExternalInput").ap()
    io["KT"] = nc.dram_tensor("KT", [3, D, T], BF16, kind="ExternalOutput").ap()
    io["V"] = nc.dram_tensor("V", [3, T, D], BF16, kind="ExternalOutput").ap()
    io["H1"] = nc.dram_tensor("H1", [T, D], F32, kind="ExternalOutput").ap()
    with contextlib.ExitStack() as st:
        P = Prog(nc)
        c = _mk_common(nc, P, st)
        outs = build_phase_a(nc, c, io)
        P.finish("sp", outs)
        P.emit()
    return nc


def core_consts(hf):
    cst = np.zeros((128, 80), np.float32)
    for g, w in enumerate(POOL_W):
        for t in range(16):
            cnt = min(t + 1, w) if hf == 0 else w
            cst[:, 16 * g + t] = 1.0 / cnt
    cst[:, 64] = 0.0 if hf == 0 else 1.0
    cst[:, 65] = EPS
    return cst


def run_phase_a(inputs):
    x = np.asarray(inputs["x"], np.float32)
    in_maps = []
    for core in range(8):
        b, hf = core // 2, core % 2
        xh = np.zeros((TX, D), np.float32)
        xh[TH:] = x[b, hf * T:(hf + 1) * T]
        if hf == 1:
            xh[:TH] = x[b, T - TH:T]
        m = {"xh": xh, "cst": core_consts(hf), "ident": np.eye(128, dtype=np.float32)}
        for k in ("w_in_a", "w_grp_a", "scale_a", "w_out_a", "w_kv", "ln_g", "ln_b"):
            m[k] = np.ascontiguousarray(np.asarray(inputs[k], np.float32))
        in_maps.append(m)
    nc = make_nc_a()
    res = run_bass_kernel_spmd(nc, in_maps, core_ids=list(range(8)))
    return res.results


def sl(start, count, step):
    return slice(start, start + (count - 1) * step + 1, step)


def unit_blocks(g, u):
    if g == 0:
        return 1, 8, 0, 8 * u
    if g == 1:
        return 2, 4, 2 * u, 0
    return 8, 1, 8 * u, 0


def build_phase_b(nc, c, io, hin, houts):
    P = c.P
    sb = c.sb
    hT = c.hT[:, :, TH:TX]
    GTb = sb("bGT", [128, 8, T], BF16)
    QT = [sb("bQT%d" % i, [128, 2, T], BF16) for i in range(3)]
    SGb = sb("bSG", [128, T], BF16)
    ACC = sb("bACC", [128, 2, T], F32)
    KB = [sb("bKB%d" % i, [128, 2048], BF16) for i in range(2)]
    VB = [sb("bVB%d" % i, [128, 2048], BF16) for i in range(2)]
    NPT, LAG = 4, 2
    PTs = [sb("bPT%d" % i, [128, 512], BF16) for i in range(NPT)]
    pend2 = []
    blk_i = [0]
    NHS = 4
    Hs = [sb("bHs%d" % i, [128, 1024], F32) for i in range(NHS)]
    Gb = sb("bGb", [128, 1024], F32)
    Bb = sb("bBb", [128, 1024], F32)
    ident = sb("bident", [128, 128], BF16)
    ETb = [sb("bET%d" % i, [128, 2, 512], F32) for i in range(2)]
    ones64 = sb("bones", [128, 64], BF16)

    RhT = c.RhT[1:]
    RGT = [Res("bGT%d" % i) for i in range(8)]
    RQT = [Res("bQT%d" % i) for i in range(3)]
    RSG, RACC = Res("bSG"), Res("bACC")
    RKBo = [Res("kbo0"), Res("kbo1")]
    RKBp = [Res("kbp0"), Res("kbp1")]
    RVBo = [[Res("vbo0a"), Res("vbo0b")], [Res("vbo1a"), Res("vbo1b")]]
    RVBp = [Res("vbp0"), Res("vbp1")]
    RPTs = [Res("pts%d" % i) for i in range(6)]
    RHs = [Res("hs%d" % i) for i in range(4)]
    RGB = Res("bGB")
    Rhb = [Res("bhb0"), Res("bhb1")]
    Rsm = [Res("bsm0"), Res("bsm1")]
    Rid, Rmask, RKA, Rones = Res("bid"), Res("bmask"), Res("bKA"), Res("bones")
    RQA = [Res("qa0"), Res("qa1")]

    P.dma("pool", ident[:], io["ident"], writes=[Rid], key="bc0")
    RET = [Res("et0"), Res("et1")]
    et_state = {"k": 0, "key": [None, None]}
    P.add("dve", lambda e: e.memset(ones64[:], 1.0), writes=[Rones])
    for g in range(3):
        P.add("dve", lambda e, g=g: e.memset(QT[g][64:128, 0, :], 0.0), writes=[RQT[g]])
        P.add("dve", lambda e, g=g: e.memset(QT[g][0:64, 1, :], 0.0), writes=[RQT[g]])

    tiles = [(i, 128, 128 * i) for i in range(NTT)]
    segs = [(512 * s_, 512, [4 * s_ + q for q in range(4)]) for s_ in range(T // 512)]

    kvi = [0]
    lnp = LNPipe(c, "lnB", Gb, Bb, RGB, hT, ident, Rid)

    for l in range(2):
        P.dma("sp", Gb[:], io["ln_g"][2 + l:3 + l, :].partition_broadcast(128).rearrange("p o d -> p (o d)"), writes=[RGB], key="bgb")
        P.dma("sp", Bb[:], io["ln_b"][2 + l:3 + l, :].partition_broadcast(128).rearrange("p o d -> p (o d)"), writes=[RGB], key="bgb")
        w_in = io["w_in_b"][l].rearrange("(k p) (g c) -> p k g c", p=128, g=4)

        def load_j(jj, w_in=w_in):
            return (wload2(c, [w_in[:, :, 0, 128 * jj:128 * (jj + 1)], w_in[:, :, 1, 128 * jj:128 * (jj + 1)]]),
                    wload2(c, [w_in[:, :, 2, 128 * jj:128 * (jj + 1)], w_in[:, :, 3, 128 * jj:128 * (jj + 1)]]))

        for j in range(8):
            if j == 0:
                nxt = load_j(0)
            (wA, rwA), (wB, rwB) = nxt
            if j < 7:
                nxt = load_j(j + 1)
            else:
                w_out_pre = io["w_out_b"][l].rearrange("(k p) c -> p k c", p=128)
                wos_pre = [wload(c, w_out_pre[:, :, 256 * q:256 * (q + 1)], lambda t: t.rearrange("p (k c) -> p k c", k=8)) for q in range(2)]
            for g in range(4):
                wt, rwt, gi = (wA, rwA, g) if g < 2 else (wB, rwB, g - 2)
                for (c0, n, tids) in segs:
                    pst, rps = psum_next(c)
                    for k in range(8):
                        P.add("pe", lambda e, k=k, c0=c0, pst=pst, wt=wt, gi=gi: e.matmul(pst[:, :], lhsT=wt[:, k, gi, :], rhs=hT[:, k, c0:c0 + 512], start=(k == 0), stop=(k == 7)),
                              reads=rwt + [RhT[t] for t in tids], writes=[rps])
                    s_ = c0 // 512
                    if g == 3:
                        P.add("act", lambda e, pst=pst, c0=c0: e.activation(out=SGb[:, c0:c0 + 512], in_=pst[:, :], func=AF.Silu), reads=[rps], writes=[RSG])
                    else:
                        d = DIL[g]
                        nb = T // (128 * d)
                        for hh in range(2):
                            hp = slice(64 * hh, 64 * hh + 64)
                            if d == 1:
                                o, i_ = QT[g][hp, hh, c0:c0 + 512], pst[hp, :]
                            elif d == 4:
                                o = QT[g][hp, hh, :].rearrange("p (r n i) -> p r n i", r=4, n=nb)[:, :, s_, :]
                                i_ = pst[hp, :].rearrange("p (i r) -> p r i", r=4)
                            else:
                                o = QT[g][hp, hh, :].rearrange("p (r i) -> p r i", r=16)[:, :, 32 * s_:32 * (s_ + 1)]
                                i_ = pst[hp, :].rearrange("p (i r) -> p r i", r=16)
                            P.add("act", lambda e, o=o, i_=i_: e.activation(out=o, in_=i_, func=AF.Copy, scale=0.125), reads=[rps], writes=[RQT[g]])
            uorder = [(g_, u_) for g_ in range(3) for u_ in range(2)]
            acc_copy = True
            if l == 0 and j == 0:
                uorder = [(0, 1), (2, 0), (2, 1), (1, 0), (1, 1), (0, 0)]
                acc_copy = False
                P.add("dve", lambda e: e.memset(ACC[:, :, :], 0.0), writes=[RACC])
            for (g, u) in uorder:
                d = DIL[g]
                nb = T // (128 * d)
                if True:
                    R_, nbk, r0, n0 = unit_blocks(g, u)
                    kb = kvi[0] % 2
                    kvi[0] += 1
                    rows = slice(128 * j, 128 * (j + 1))
                    KBv = KB[kb][:, 0:R_ * (nbk + 1) * 128].rearrange("p (r b i) -> p r b i", r=R_, b=nbk + 1)
                    VBv = VB[kb][:, 0:R_ * (nbk + 1) * 128].rearrange("p (r b i) -> p r b i", r=R_, b=nbk + 1)
                    W_ = (nbk + 1) * 128
                    KBf = KB[kb][:, 0:R_ * W_].rearrange("p (r c) -> p r c", r=R_)
                    ksrc = io["KT"][g, rows, :].rearrange("p (r c) -> p r c", r=d)[:, r0:r0 + R_, n0 * 128:(n0 + nbk) * 128]
                    P.dma("sp", KBf[:, :, 128:W_], ksrc, writes=[RKBo[kb]], key="kbo%d" % kb)
                    vall = io["V"][g].rearrange("(n i dd) f -> i dd n f", i=128, dd=d)
                    if nbk == 1:
                        P.dma("sp", VBv[:, :, 1, :], vall[:, r0:r0 + R_, n0, rows], writes=[RVBo[kb][0]], key="vbo%d_0" % kb)
                    else:
                        for ri in range(R_):
                            P.dma("sp", VBv[:, ri, 1:1 + nbk, :], vall[:, r0 + ri, n0:n0 + nbk, rows], writes=[RVBo[kb][ri]], key="vbo%d_%d" % (kb, ri))
                    if n0 > 0:
                        P.dma("sp", KBv[:, 0, 0, :], io["KT"][g, rows, (n0 - 1) * 128:n0 * 128], writes=[RKBp[kb]], key="kbp%d" % kb)
                        P.dma("sp", VBv[:, 0, 0, :], io["V"][g, (n0 - 1) * 128:n0 * 128, rows], writes=[RVBp[kb]], key="vbp%d" % kb)
                    else:
                        kps = io["KTp_fn"](g, r0, R_)[rows, :].rearrange("p (r i) -> p r i", i=128)
                        vps = io["Vp_fn"](g, r0, R_)[:, rows].rearrange("(r i) f -> i r f", i=128)
                        P.dma("sp", KBv[:, :, 0, :], kps, reads=io["prev_res"](g, r0, 1), writes=[RKBp[kb]], key="kbp%d" % kb)
                        P.dma("sp", VBv[:, :, 0, :], vps, reads=io["prev_res"](g, r0, 0), writes=[RVBp[kb]], key="vbp%d" % kb)
                    kvres = [RKBo[kb], RKBp[kb]]
                    vvres = RVBo[kb] + [RVBp[kb]]
                    if (j, g) in et_state["key"]:
                        eb = et_state["key"].index((j, g))
                    else:
                        eb = et_state["k"] % 2
                        et_state["k"] += 1
                        et_state["key"][eb] = (j, g)
                        P.dma("sp", ETb[eb][:], io["ET"][j, g].rearrange("v c x -> c v x"), writes=[RET[eb]], key="et%d" % eb)
                    for ri in range(R_):
                        for bi in range(nbk):
                            r = r0 + ri
                            n = n0 + bi
                            qcol = (r * nb + n) * 128
                            pb = blk_i[0] % NPT
                            blk_i[0] += 1

                            def stage1(g=g, n=n, ri=ri, bi=bi, qcol=qcol, pb=pb, KBv=KBv, kvres=kvres, eb=eb):
                                pst, rps = psum_next(c)
                                for ch in range(2):
                                    oc_ = ch * 256
                                    P.add("pe", lambda e, oc_=oc_, ch=ch: e.matmul(pst[:, oc_:oc_ + 256], lhsT=KBv[:, ri, bi + ch, :], rhs=QT[g][:, :, qcol:qcol + 128], start=True, stop=True),
                                          reads=kvres + [RQT[g]], writes=[rps])
                                P.add("act", lambda e: e.activation(out=PTs[pb][:, :], in_=pst[:, :], func=AF.Exp), reads=[rps], writes=[RPTs[pb]])
                                var = 0 if n == 0 else 1
                                meng = "pool" if (pb % 2 == 0) else "dve"
                                P.add(meng, lambda e: e.tensor_tensor(out=PTs[pb][:, :], in0=PTs[pb][:, :], in1=ETb[eb][:, var, :], op=ALU.mult),
                                      reads=[RPTs[pb], RET[eb]], writes=[RPTs[pb]])

                            def stage2(g=g, n=n, r=r, d=d, ri=ri, bi=bi, pb=pb, VBv=VBv, vvres=vvres, acc_copy=acc_copy):
                                pnd, rpnd = psum_next(c)
                                for hh in range(2):
                                    hp = slice(64 * hh, 64 * hh + 64)
                                    for ch in range(2):
                                        oc_ = (ch * 2 + hh) * 128
                                        P.add("pe", lambda e, hp=hp, ch=ch, oc_=oc_: e.matmul(pnd[hp, 0:128], lhsT=VBv[:, ri, bi + ch, hp], rhs=PTs[pb][:, oc_:oc_ + 128], start=(ch == 0), stop=(ch == 1)),
                                              reads=vvres + [RPTs[pb]], writes=[rpnd])
                                    for ch in range(2):
                                        oc_ = (ch * 2 + hh) * 128
                                        P.add("pe", lambda e, hp=hp, oc_=oc_, ch=ch: e.matmul(pnd[hp, 128:256], lhsT=ones64[:, :], rhs=PTs[pb][:, oc_:oc_ + 128], start=(ch == 0), stop=(ch == 1)),
                                              reads=[Rones, RPTs[pb]], writes=[rpnd])
                                av = ACC[:, :, sl(128 * n * d + r, 128, d)]
                                pv = pnd[:, 0:256].rearrange("p (x i) -> p x i", x=2)
                                if g == 0 and acc_copy:
                                    P.add("dve", lambda e: e.tensor_copy(out=av, in_=pv), reads=[rpnd], writes=[RACC])
                                else:
                                    P.add("dve", lambda e: e.tensor_tensor(out=av, in0=av, in1=pv, op=ALU.add), reads=[rpnd, RACC], writes=[RACC])

                            stage1()
                            pend2.append(stage2)
                            if len(pend2) > LAG:
                                pend2.pop(0)()
            for f2 in pend2:
                f2()
            del pend2[:]
            P.add("act", lambda e: e.activation(out=ACC[:, 1, :], in_=ACC[:, 1, :], func=AF.Ln), reads=[RACC], writes=[RACC])
            P.add("act", lambda e: e.activation(out=ACC[:, 1, :], in_=ACC[:, 1, :], func=AF.Exp, scale=-1.0), reads=[RACC], writes=[RACC])
            P.add("dve", lambda e: e.tensor_tensor(out=ACC[:, 0, :], in0=ACC[:, 0, :], in1=ACC[:, 1, :], op=ALU.mult), reads=[RACC], writes=[RACC])
            P.add("dve", lambda e, j=j: e.tensor_tensor(out=GTb[:, j, :], in0=ACC[:, 0, :], in1=SGb[:, :], op=ALU.mult), reads=[RACC, RSG], writes=[RGT[j]])
        w_out = io["w_out_b"][l].rearrange("(k p) c -> p k c", p=128)
        wos = wos_pre + [wload(c, w_out[:, :, 256 * q:256 * (q + 1)], lambda t: t.rearrange("p (k c) -> p k c", k=8)) for q in range(2, 4)]
        hsrc = hin if l == 0 else houts[0]
        stores = []
        for (ti, rows_, col0) in tiles:
            b = ti % NHS
            P.dma("sp", Hs[b][:], hsrc[128 * ti:128 * (ti + 1), :], writes=[RHs[b]], key="hs%d" % b)
            for q in range(4):
                wo, rwo = wos[q]
                pst, rps = psum_next(c)
                for k in range(8):
                    P.add("pe", lambda e, k=k, pst=pst, col0=col0, wo=wo: e.matmul(pst[:, 0:256], lhsT=GTb[:, k, col0:col0 + 128], rhs=wo[:, k, :], start=(k == 0), stop=(k == 7)),
                          reads=[rwo, RGT[k]], writes=[rps])
                hs = Hs[b][:, 256 * q:256 * (q + 1)]
                P.add("dve", lambda e, hs=hs, pst=pst: e.scalar_tensor_tensor(out=hs, in0=hs, scalar=ALPHA, in1=pst[:, 0:256], op0=ALU.mult, op1=ALU.add), reads=[rps, RHs[b]], writes=[RHs[b]])

            def after(ti=ti, b=b, l=l):
                stores.append(P.dma("sp", houts[l][128 * ti:128 * (ti + 1), :], Hs[b][:], reads=[RHs[b]], key="ho%d" % b))

            lnp.push(Hs[b], RHs[b], 128, col0=col0, rhT=RhT[ti], after=after, tr=(l == 0))
        lnp.flush()
    return stores


def make_nc_b():
    nc = bass.Bass("TRN2", target_bir_lowering=False)
    io = {}
    io["H1"] = nc.dram_tensor("H1", [T, D], F32, kind="ExternalInput").ap()
    io["KT"] = nc.dram_tensor("KT", [3, D, T], BF16, kind="ExternalInput").ap()
    io["V"] = nc.dram_tensor("V", [3, T, D], BF16, kind="ExternalInput").ap()
    io["KTp"] = nc.dram_tensor("KTp", [3, D, 2048], BF16, kind="ExternalInput").ap()
    io["Vp"] = nc.dram_tensor("Vp", [3, 2048, D], BF16, kind="ExternalInput").ap()
    io["w_in_b"] = nc.dram_tensor("w_in_b", [2, D, 4096], F32, kind="ExternalInput").ap()
    io["w_out_b"] = nc.dram_tensor("w_out_b", [2, D, D], F32, kind="ExternalInput").ap()
    io["ln_g"] = nc.dram_tensor("ln_g", [4, D], F32, kind="ExternalInput").ap()
    io["ln_b"] = nc.dram_tensor("ln_b", [4, D], F32, kind="ExternalInput").ap()
    io["ident"] = nc.dram_tensor("ident", [128, 128], F32, kind="ExternalInput").ap()
    io["KA"] = nc.dram_tensor("KA", [7, 128], F32, kind="ExternalInput").ap()
    io["QA"] = nc.dram_tensor("QA", [8, 7, 9 * 2 * 128], F32, kind="ExternalInput").ap()
    h2 = nc.dram_tensor("H2", [T, D], F32, kind="Internal").ap()
    out = nc.dram_tensor("out", [T, D], F32, kind="ExternalOutput").ap()
    with contextlib.ExitStack() as st:
        P = Prog(nc)
        c = _mk_common(nc, P, st)
        stores = build_phase_b(nc, c, io, io["H1"], [h2, out])
        P.finish("sp", stores)
        P.emit()
    return nc


def _split3(x):
    x = np.asarray(x, np.float64)
    hi = x.astype(ml_dtypes.bfloat16).astype(np.float64)
    lo = (x - hi).astype(ml_dtypes.bfloat16).astype(np.float64)
    lo2 = (x - hi - lo).astype(ml_dtypes.bfloat16).astype(np.float64)
    return hi, lo, lo2


def attn_consts(hf):
    a = np.arange(128, dtype=np.int64)[None, :]
    cc = np.arange(128, dtype=np.int64)[:, None]
    i = np.arange(1, 49, dtype=np.float32)
    slopes = (np.float32(2.0) ** (np.float32(-8.0) * i / np.float32(48))).astype(np.float32).reshape(3, 16)
    ET = np.zeros((8, 3, 2, 128, 2, 2, 128), np.float32)
    rel_p = 128 + a - cc
    rel_c = a - cc
    val_p = rel_p <= 128
    val_c = rel_c >= 0
    for j in range(8):
        for g in range(3):
            for hh in range(2):
                sl_ = slopes[g, 2 * j + hh]
                bp = -(sl_ * (DIL[g] * rel_p).astype(np.float32)).astype(np.float32)
                bc = -(sl_ * (DIL[g] * rel_c).astype(np.float32)).astype(np.float32)
                ep = np.where(val_p, np.exp(bp.astype(np.float64)), 0.0).astype(np.float32)
                ec = np.where(val_c, np.exp(bc.astype(np.float64)), 0.0).astype(np.float32)
                for var in range(2):
                    ET[j, g, var, :, 0, hh, :] = 0.0 if (var == 0 and hf == 0) else ep
                    ET[j, g, var, :, 1, hh, :] = ec
    return ET.reshape(8, 3, 2, 128, 512)


def run_phase_b(inputs, resA):
    in_maps = []
    bf = ml_dtypes.bfloat16
    for core in range(8):
        b, hf = core // 2, core % 2
        r = resA[core]
        KTp = np.zeros((3, D, 2048), bf)
        Vp = np.zeros((3, 2048, D), bf)
        if hf == 1:
            pr = resA[core - 1]
            for g in range(3):
                d = DIL[g]
                nb = T // (128 * d)
                kt = np.asarray(pr["KT"][g]).reshape(D, d, nb, 128)
                KTp[g, :, 0:d * 128] = kt[:, :, nb - 1, :].reshape(D, d * 128)
                v = np.asarray(pr["V"][g]).reshape(nb, 128, d, D)
                Vp[g, 0:d * 128, :] = v[nb - 1].transpose(1, 0, 2).reshape(d * 128, D)
        masks, KA, QA = attn_consts(hf)
        m = {"H1": np.asarray(r["H1"]), "KT": np.asarray(r["KT"]), "V": np.asarray(r["V"]), "KTp": KTp, "Vp": Vp,
             "ident": np.eye(128, dtype=np.float32), "masks": masks, "KA": KA, "QA": QA}
        for k in ("w_in_b", "w_out_b", "ln_g", "ln_b"):
            m[k] = np.ascontiguousarray(np.asarray(inputs[k], np.float32))
        in_maps.append(m)
    nc = make_nc_b()
    res = run_bass_kernel_spmd(nc, in_maps, core_ids=list(range(8)))
    return res.results


NCH = 8
TL_SHAPE = {0: (512, 512), 3: (1024, 256), 1: (1024, 512), 2: (1024, 512), 4: (1024, 512), 5: (1024, 512), 6: (128, 512), 7: (1024, 64)}


def tail_loc(g, r0):
    if g == 0:
        return 6, 7, 0
    if g == 1:
        return 0, 3, 128 * r0
    return (1, 4, 128 * r0) if r0 < 8 else (2, 5, 128 * (r0 - 8))


def make_nc_fused():
    nc = bass.Bass("TRN2", target_bir_lowering=False)
    io = {}
    io["xh"] = nc.dram_tensor("xh", [TX, D], F32, kind="ExternalInput").ap()
    io["w_in_a"] = nc.dram_tensor("w_in_a", [2, D, 2 * DA], F32, kind="ExternalInput").ap()
    io["w_grp_a"] = nc.dram_tensor("w_grp_a", [2, 4, 512, 512], F32, kind="ExternalInput").ap()
    io["scale_a"] = nc.dram_tensor("scale_a", [2, DA], F32, kind="ExternalInput").ap()
    io["w_out_a"] = nc.dram_tensor("w_out_a", [2, DA, D], F32, kind="ExternalInput").ap()
    io["w_kv"] = nc.dram_tensor("w_kv", [D, 6144], F32, kind="ExternalInput").ap()
    io["w_in_b"] = nc.dram_tensor("w_in_b", [2, D, 4096], F32, kind="ExternalInput").ap()
    io["w_out_b"] = nc.dram_tensor("w_out_b", [2, D, D], F32, kind="ExternalInput").ap()
    io["ln_g"] = nc.dram_tensor("ln_g", [4, D], F32, kind="ExternalInput").ap()
    io["ln_b"] = nc.dram_tensor("ln_b", [4, D], F32, kind="ExternalInput").ap()
    io["cst"] = nc.dram_tensor("cst", [128, 80], F32, kind="ExternalInput").ap()
    io["ident"] = nc.dram_tensor("ident", [128, 128], F32, kind="ExternalInput").ap()
    io["ET"] = nc.dram_tensor("ET", [8, 3, 2, 128, 512], F32, kind="ExternalInput").ap()
    io["KT"] = nc.dram_tensor("KT", [3, D, T], BF16, kind="Internal").ap()
    io["V"] = nc.dram_tensor("V", [3, T, D], BF16, kind="Internal").ap()
    io["H1"] = nc.dram_tensor("H1", [T, D], F32, kind="Internal").ap()
    h2 = nc.dram_tensor("H2", [T, D], F32, kind="Internal").ap()
    tl32 = [nc.dram_tensor("TL%d" % i, list(TL_SHAPE[i]), F32, kind="Internal").ap() for i in range(NCH)]
    ga32 = [nc.dram_tensor("GA%d" % i, [2 * TL_SHAPE[i][0], TL_SHAPE[i][1]], F32, kind="Internal").ap() for i in range(NCH)]
    out = nc.dram_tensor("out", [T, D], F32, kind="ExternalOutput").ap()
    TL = [t.bitcast(BF16) for t in tl32]
    GA = [t.bitcast(BF16) for t in ga32]
    with contextlib.ExitStack() as st:
        P = Prog(nc)
        c = _mk_common(nc, P, st)
        rga = [Res("GA%d" % i) for i in range(NCH)]
        io["tail"] = {"TL": TL, "tl32": tl32, "ga32": ga32, "rtl": [[] for _ in range(NCH)], "rga": rga}
        with contextlib.ExitStack() as stA:
            c.cur = stA
            build_phase_a(nc, c, io)
        P.barrier(exclude=["cc%d" % i for i in range(NCH)])

        def vp_fn(g, r0, R_):
            cv, ck, off = tail_loc(g, r0)
            return GA[cv][off:off + 128 * R_, :]

        def ktp_fn(g, r0, R_):
            cv, ck, off = tail_loc(g, r0)
            return GA[ck][0:1024, off:off + 128 * R_]

        def prev_res(g, r0, is_k):
            cv, ck, off = tail_loc(g, r0)
            return [rga[ck if is_k else cv]]

        io["Vp_fn"], io["KTp_fn"], io["prev_res"] = vp_fn, ktp_fn, prev_res
        with contextlib.ExitStack() as stB:
            c.cur = stB
            stores = build_phase_b(nc, c, io, io["H1"], [h2, out])
        P.finish("sp", stores)
        P.emit()
    return nc


_NC_CACHE = {}


def kernel(**inputs):
    x = np.asarray(inputs["x"], np.float32)
    in_maps = []
    wkeys = ("w_in_a", "w_grp_a", "scale_a", "w_out_a", "w_kv", "w_in_b", "w_out_b", "ln_g", "ln_b")
    ws = {k: np.ascontiguousarray(np.asarray(inputs[k], np.float32)) for k in wkeys}
    ident = np.eye(128, dtype=np.float32)
    ets = [attn_consts(0), attn_consts(1)]
    for core in range(8):
        b, hf = core // 2, core % 2
        xh = np.zeros((TX, D), np.float32)
        xh[TH:] = x[b, hf * T:(hf + 1) * T]
        if hf == 1:
            xh[:TH] = x[b, T - TH:T]
        m = {"xh": xh, "cst": core_consts(hf), "ident": ident, "ET": ets[hf]}
        m.update(ws)
        in_maps.append(m)
    if "nc" not in _NC_CACHE:
        _NC_CACHE["nc"] = make_nc_fused()
    res = run_bass_kernel_spmd(_NC_CACHE["nc"], in_maps, core_ids=list(range(8)))
    out = np.zeros((4, 4096, D), np.float32)
    for core in range(8):
        b, hf = core // 2, core % 2
        out[b, hf * T:(hf + 1) * T] = np.asarray(res.results[core]["out"], np.float32)
    return out
```

```python
import contextlib
import numpy as np
import ml_dtypes
import concourse.bass as bass
import concourse.mybir as mybir
from concourse.bass_utils import run_bass_kernel_spmd

F32 = mybir.dt.float32
BF16 = mybir.dt.bfloat16
AF = mybir.ActivationFunctionType
ALU = mybir.AluOpType

ENGS = ("pe", "act", "dve", "pool", "sp")


class Res:
    __slots__ = ("name", "writer", "readers")

    def __init__(self, name=""):
        self.name = name
        self.writer = None
        self.readers = []


class Op:
    __slots__ = ("eng", "fn", "deps", "marked", "idx", "is_dma", "key", "kidx", "inc")

    def __init__(self, eng, fn, is_dma=False, key=None, inc=16):
        self.inc = inc
        self.eng = eng
        self.fn = fn
        self.deps = []
        self.marked = False
        self.idx = None
        self.is_dma = is_dma
        self.key = key
        self.kidx = None


class Prog:
    def __init__(self, nc):
        self.nc = nc
        self.ops = {e: [] for e in ENGS}
        self.key_count = {}
        self.final_waits = []

    def _dep(self, op, prod):
        if prod is None or prod is op:
            return
        if (not prod.is_dma) and (not op.is_dma) and prod.eng == "pe" and op.eng == "pe":
            return
        prod.marked = True
        op.deps.append(prod)

    def barrier(self, exclude=()):
        lasts = []
        for e in ENGS:
            comp = [o for o in self.ops[e] if not o.is_dma and o.fn is not None]
            if comp:
                lasts.append(comp[-1])
        lastk = {}
        for e in ENGS:
            for o in self.ops[e]:
                if o.is_dma and o.key not in exclude:
                    lastk[o.key] = o
        lasts += list(lastk.values())
        for o in lasts:
            o.marked = True
        for e in ENGS:
            op = Op(e, None)
            op.deps = list(lasts)
            self.ops[e].append(op)

    def add(self, eng, fn, reads=(), writes=(), is_dma=False, key=None, inc=16):
        op = Op(eng, fn, is_dma, key, inc)
        if is_dma:
            self.key_count[key] = self.key_count.get(key, 0) + 1
            op.kidx = self.key_count[key]
        for r in reads:
            self._dep(op, r.writer)
        for w in writes:
            self._dep(op, w.writer)
            for rd in w.readers:
                self._dep(op, rd)
        for r in reads:
            r.readers.append(op)
        for w in writes:
            w.writer = op
            w.readers = []
        self.ops[eng].append(op)
        return op

    def dma(self, eng, out, in_, reads=(), writes=(), key=None, **kw):
        def fn(e, out=out, in_=in_, kw=kw):
            return e.dma_start(out=out, in_=in_, **kw)
        return self.add(eng, fn, reads, writes, is_dma=True, key=key)

    def finish(self, eng, ops):
        for o in ops:
            o.marked = True
        self.final_waits.append((eng, list(ops)))

    def emit(self):
        nc = self.nc
        for e in ENGS:
            c = 0
            for op in self.ops[e]:
                if op.is_dma:
                    continue
                if op.marked:
                    c += 1
                    op.idx = c
        with contextlib.ExitStack() as st:
            esem = {e: st.enter_context(nc.semaphore("s_" + e)) for e in ENGS}
            ksem = {k: st.enter_context(nc.semaphore("k_" + str(k))) for k in self.key_count}
            block = st.enter_context(nc.Block())

            def semval(prod):
                if prod.is_dma:
                    return ksem[prod.key], prod.inc * prod.kidx
                return esem[prod.eng], prod.idx

            def do_waits(eng, waited, prods):
                need = {}
                for p in prods:
                    s, v = semval(p)
                    if need.get(s, 0) < v:
                        need[s] = v
                for s, v in need.items():
                    if waited.get(s, 0) < v:
                        eng.wait_ge(s, v)
                        waited[s] = v

            def run(engname, eng):
                waited = {}
                for op in self.ops[engname]:
                    do_waits(eng, waited, op.deps)
                    if op.fn is None:
                        continue
                    ins = op.fn(eng)
                    if op.is_dma:
                        ins.then_inc(ksem[op.key], op.inc)
                    elif op.marked:
                        ins.then_inc(esem[engname], 1)
                for (fe, fops) in self.final_waits:
                    if fe == engname:
                        do_waits(eng, waited, fops)

            @block.tensor
            def _(eng):
                run("pe", eng)

            @block.scalar
            def _(eng):
                run("act", eng)

            @block.vector
            def _(eng):
                run("dve", eng)

            @block.gpsimd
            def _(eng):
                run("pool", eng)

            @block.sync
            def _(eng):
                run("sp", eng)


D = 1024
T = 2048
TH = 32
TX = T + TH
NTT = T // 128
DA = 2048
POOL_W = (2, 4, 8, 16)
ALPHA = float(8.0 ** 0.25)
EPS = 1e-5
DIL = (1, 4, 16)
NWS = 6
NPS = 7
NEG = -30000.0


class Ctx:
    pass


def _mk_common(nc, P, st):
    c = Ctx()
    c.nc, c.P, c.st = nc, P, st

    c.cur = st

    def sb(name, shape, dt):
        return c.cur.enter_context(nc.sbuf_tensor("sb_" + name, shape, dt))

    def ps(name, shape, dt):
        return st.enter_context(nc.psum_tensor("pp_" + name, shape, dt))

    c.sb, c.ps = sb, ps
    c.PS = [ps("ps%d" % i, [128, 512], F32) for i in range(NPS)]
    c.RPS = [Res("ps%d" % i) for i in range(NPS)]
    c.psi = 0
    c.PT = ps("pst", [128, 8, 128], BF16)
    c.RPT = Res("pst")
    c.WR = [sb("wr%d" % i, [128, 2048], BF16) for i in range(NWS)]
    c.RWR = [Res("wr%d" % i) for i in range(NWS)]
    c.wi = 0
    c.hT = sb("hT", [128, 8, TX], BF16)
    c.RhT = [Res("hT%d" % i) for i in range(NTT + 1)]
    return c


def psum_next(c):
    i = c.psi % NPS
    c.psi += 1
    return c.PS[i], c.RPS[i]


def wload(c, src_ap, view):
    i = c.wi % NWS
    c.wi += 1
    t = view(c.WR[i][:])
    ws = [c.RWR[i]] + ([c.RWR2[i]] if hasattr(c, "RWR2") else [])
    c.P.dma("pool", t, src_ap, writes=ws, key="w%d" % i)
    return t, c.RWR[i]


def wload2(c, srcs):
    i = c.wi % NWS
    c.wi += 1
    t = c.WR[i][:].rearrange("p (k g c) -> p k g c", k=8, g=2)
    if not hasattr(c, "RWR2"):
        c.RWR2 = [Res("wrb%d" % q) for q in range(NWS)]
    rs = [c.RWR[i], c.RWR2[i]]
    for gi, src in enumerate(srcs):
        c.P.dma("pool", t[:, :, gi, :], src, writes=[rs[gi]], key=("w%d" % i) if gi == 0 else ("w%db" % i))
    return t, rs


def emit_ln_tile(c, Htile, rH, npart, Gb, Bb, rGB, hb, rhb, sm, rsm, eps_t):
    P = c.P
    stats, mv, rstd, nmr = sm
    P.add("dve", lambda e: e.bn_stats(out=stats[0:npart, 0, :], in_=Htile[0:npart, 0:512]), reads=[rH], writes=[rsm])
    P.add("dve", lambda e: e.bn_stats(out=stats[0:npart, 1, :], in_=Htile[0:npart, 512:1024]), reads=[rH], writes=[rsm])
    P.add("dve", lambda e: e.bn_aggr(out=mv[0:npart, :], in_=stats[0:npart, :, :]), reads=[rsm], writes=[rsm])
    P.add("dve", lambda e: e.tensor_scalar(out=rstd[0:npart, :], in0=mv[0:npart, 1:2], scalar1=EPS, scalar2=None, op0=ALU.add), reads=[rsm], writes=[rsm])
    P.add("act", lambda e: e.sqrt(out=rstd[0:npart, :], in_=rstd[0:npart, :]), reads=[rsm], writes=[rsm])
    P.add("dve", lambda e: e.reciprocal(out=rstd[0:npart, :], in_=rstd[0:npart, :]), reads=[rsm], writes=[rsm])
    P.add("dve", lambda e: e.scalar_tensor_tensor(out=nmr[0:npart, :], in0=mv[0:npart, 0:1], scalar=-1.0, in1=rstd[0:npart, :], op0=ALU.mult, op1=ALU.mult), reads=[rsm], writes=[rsm])
    P.add("act", lambda e: e.activation(out=Htile[0:npart, :], in_=Htile[0:npart, :], func=AF.Identity, bias=nmr[0:npart, :], scale=rstd[0:npart, :]), reads=[rH, rsm], writes=[rH])
    P.add("dve", lambda e: e.tensor_tensor(out=Htile[0:npart, :], in0=Htile[0:npart, :], in1=Gb[0:npart, :], op=ALU.mult), reads=[rH, rGB], writes=[rH])
    P.add("dve", lambda e: e.tensor_tensor(out=Htile[0:npart, :], in0=Htile[0:npart, :], in1=Bb[0:npart, :], op=ALU.add), reads=[rH, rGB], writes=[rH])
    P.add("act", lambda e: e.copy(out=hb[0:npart, :], in_=Htile[0:npart, :]), reads=[rH], writes=[rhb])


def emit_transpose_tile(c, hb, rhb, npart, hT, col0, rhT, ident, rid):
    P = c.P
    PT, RPT = c.PT, c.RPT
    for k in range(8):
        P.add("pe", lambda e, k=k: e.transpose(out=PT[:, k, 0:npart], in_=hb[0:npart, k * 128:(k + 1) * 128], identity=ident[0:npart, 0:npart]),
              reads=[rhb, rid], writes=[RPT])
    P.add("act", lambda e: e.copy(out=hT[:, :, col0:col0 + npart], in_=PT[:, :, 0:npart]), reads=[RPT], writes=[rhT])


class LNPipe:
    NSM = 4
    NHB = 3

    def __init__(self, c, prefix, Gb, Bb, rGB, hT, ident, rid, do_transpose=True):
        self.c, self.Gb, self.Bb, self.rGB = c, Gb, Bb, rGB
        self.hT, self.ident, self.rid = hT, ident, rid
        self.do_transpose = do_transpose
        sb = c.sb
        self.sm = [(sb(prefix + "st%d" % i, [128, 2, 6], F32), sb(prefix + "mv%d" % i, [128, 2], F32), sb(prefix + "rs%d" % i, [128, 1], F32), sb(prefix + "nm%d" % i, [128, 1], F32)) for i in range(self.NSM)]
        self.rsm = [Res(prefix + "sm%d" % i) for i in range(self.NSM)]
        self.hb = [sb(prefix + "hb%d" % i, [128, 1024], BF16) for i in range(self.NHB)]
        self.rhb = [Res(prefix + "hb%d" % i) for i in range(self.NHB)]
        self.n = 0
        self.q = []

    def _s1(self, x):
        P = self.c.P
        Ht, rH, npart = x["H"], x["rH"], x["np"]
        stats, mv, rstd, nmr = self.sm[x["si"]]
        rsm = self.rsm[x["si"]]
        P.add("dve", lambda e: e.bn_stats(out=stats[0:npart, 0, :], in_=Ht[0:npart, 0:512]), reads=[rH], writes=[rsm])
        P.add("dve", lambda e: e.bn_stats(out=stats[0:npart, 1, :], in_=Ht[0:npart, 512:1024]), reads=[rH], writes=[rsm])
        P.add("dve", lambda e: e.bn_aggr(out=mv[0:npart, :], in_=stats[0:npart, :, :]), reads=[rsm], writes=[rsm])
        P.add("dve", lambda e: e.tensor_scalar(out=rstd[0:npart, :], in0=mv[0:npart, 1:2], scalar1=EPS, scalar2=None, op0=ALU.add), reads=[rsm], writes=[rsm])
        P.add("act", lambda e: e.sqrt(out=rstd[0:npart, :], in_=rstd[0:npart, :]), reads=[rsm], writes=[rsm])

    def _s2(self, x):
        P = self.c.P
        Ht, rH, npart = x["H"], x["rH"], x["np"]
        stats, mv, rstd, nmr = self.sm[x["si"]]
        rsm = self.rsm[x["si"]]
        P.add("dve", lambda e: e.reciprocal(out=rstd[0:npart, :], in_=rstd[0:npart, :]), reads=[rsm], writes=[rsm])
        P.add("dve", lambda e: e.scalar_tensor_tensor(out=nmr[0:npart, :], in0=mv[0:npart, 0:1], scalar=-1.0, in1=rstd[0:npart, :], op0=ALU.mult, op1=ALU.mult), reads=[rsm], writes=[rsm])
        P.add("act", lambda e: e.activation(out=Ht[0:npart, :], in_=Ht[0:npart, :], func=AF.Identity, bias=nmr[0:npart, :], scale=rstd[0:npart, :]), reads=[rH, rsm], writes=[rH])

    def _s3(self, x):
        P = self.c.P
        Ht, rH, npart = x["H"], x["rH"], x["np"]
        Gb, Bb, rGB = self.Gb, self.Bb, self.rGB
        hb, rhb = self.hb[x["hi"]], self.rhb[x["hi"]]
        P.add("pool", lambda e: e.tensor_tensor(out=Ht[0:npart, :], in0=Ht[0:npart, :], in1=Gb[0:npart, :], op=ALU.mult), reads=[rH, rGB], writes=[rH])
        P.add("pool", lambda e: e.tensor_tensor(out=Ht[0:npart, :], in0=Ht[0:npart, :], in1=Bb[0:npart, :], op=ALU.add), reads=[rH, rGB], writes=[rH])
        if self.do_transpose and x.get("tr", True):
            P.add("act", lambda e: e.copy(out=hb[0:npart, :], in_=Ht[0:npart, :]), reads=[rH], writes=[rhb])
        if x.get("after") is not None:
            x["after"]()

    def _s4(self, x):
        if self.do_transpose and x.get("tr", True):
            emit_transpose_tile(self.c, self.hb[x["hi"]], self.rhb[x["hi"]], x["np"], self.hT, x["col0"], x["rhT"], self.ident, self.rid)

    def _advance(self):
        stages = [self._s1, self._s2, self._s3, self._s4]
        for item in self.q:
            stages[item[0]](item[1])
            item[0] += 1
        self.q = [it for it in self.q if it[0] < 4]

    def push(self, Htile, rH, npart, col0=None, rhT=None, after=None, tr=True):
        x = {"H": Htile, "rH": rH, "np": npart, "col0": col0, "rhT": rhT, "after": after, "tr": tr,
             "si": self.n % self.NSM, "hi": self.n % self.NHB}
        self.n += 1
        self.q.append([0, x])
        self._advance()

    def flush(self):
        while self.q:
            self._advance()


def build_phase_a(nc, c, io):
    P = c.P
    sb = c.sb
    H = sb("H", [128, NTT + 1, 1024], F32)
    hT = c.hT
    U = [sb("U%d" % i, [128, 16 + TX], F32) for i in range(2)]
    A_ = sb("poolA", [128, 560], F32)
    B_ = sb("poolB", [128, 560], F32)
    PTb = sb("pooledT", [128, 4, TX], BF16)
    SG = [sb("SG%d" % i, [128, TX], BF16) for i in range(3)]
    GT = sb("GT", [128, 4, TX], BF16)
    Gb = sb("Gb", [128, 1024], F32)
    Bb = sb("Bb", [128, 1024], F32)
    scl = sb("scl", [128, 2, 16], F32)
    cst = sb("cst", [128, 80], F32)
    ident = sb("ident", [128, 128], BF16)
    tmp16 = sb("tmp16", [128, 16], F32)

    RH = [Res("H%d" % i) for i in range(NTT + 1)]
    RhT = c.RhT
    RU = [Res("U0"), Res("U1")]
    RA, RB = Res("A"), Res("B")
    RPTb = [Res("pooledT%d" % i) for i in range(4)]
    RSG = [Res("SG0"), Res("SG1"), Res("SG2")]
    RGT = [Res("GT%d" % i) for i in range(4)]
    RGB = Res("GB")
    Rscl, Rcst, Rid, Rt16 = Res("scl"), Res("cst"), Res("ident"), Res("t16")
    Rzero = Res("zero")

    hflag = cst[:, 64:65]
    eps_t = cst[:, 65:66]

    P.dma("sp", cst[:], io["cst"], writes=[Rcst], key="c0")
    P.dma("pool", ident[:], io["ident"], writes=[Rid], key="c1")
    P.dma("sp", scl[:], io["scale_a"].rearrange("l (c p) -> p l c", p=128), writes=[Rscl], key="c2", allow_slow_non_contiguous=True)
    P.dma("sp", H[0:TH, 0, :], io["xh"][0:TH, :], writes=[RH[0]], key="x0")
    for i in range(NTT):
        P.dma("sp", H[:, 1 + i, :], io["xh"][TH + 128 * i: TH + 128 * (i + 1), :], writes=[RH[1 + i]], key="x%d" % (1 + i))
    for b in range(2):
        P.add("dve", lambda e, b=b: e.memset(U[b][:, 0:16], 0.0), writes=[RU[b]])

    tiles = [(0, TH, 0)] + [(1 + i, 128, TH + 128 * i) for i in range(NTT)]
    segs = [(0, TH, [0])] + [(TH + 512 * s, 512, [1 + 4 * s + q for q in range(4)]) for s in range(T // 512)]

    lnp = LNPipe(c, "lnA", Gb, Bb, RGB, hT, ident, Rid)
    for (ti, rows, col0) in tiles:
        b = ti % lnp.NHB
        P.add("act", lambda e, ti=ti, rows=rows, b=b: e.copy(out=lnp.hb[b][0:rows, :], in_=H[0:rows, ti, :]), reads=[RH[ti]], writes=[lnp.rhb[b]])
        emit_transpose_tile(c, lnp.hb[b], lnp.rhb[b], rows, hT, col0, RhT[ti], ident, Rid)

    def linear_fm(wt, rwt, ocol, kch, rhs_fn, rhs_res_fn, seglist, evac):
        for (c0, n, tids) in seglist:
            pst, rps = psum_next(c)
            for k in range(kch):
                P.add("pe", lambda e, k=k, c0=c0, n=n, pst=pst: e.matmul(pst[:, 0:n], lhsT=wt[:, k, ocol:ocol + 128], rhs=rhs_fn(k, c0, n), start=(k == 0), stop=(k == kch - 1)),
                      reads=[rwt] + rhs_res_fn(c0, n, tids), writes=[rps])
            evac(pst, rps, c0, n, tids)

    for l in range(2):
        last = (l == 1)
        P.dma("sp", Gb[:], io["ln_g"][l:l + 1, :].partition_broadcast(128).rearrange("p o d -> p (o d)"), writes=[RGB], key="gb")
        P.dma("sp", Bb[:], io["ln_b"][l:l + 1, :].partition_broadcast(128).rearrange("p o d -> p (o d)"), writes=[RGB], key="gb")
        w_in = io["w_in_a"][l].rearrange("(k p) c -> p k c", p=128)
        segs_own = segs[1:]
        segs_g = segs if not last else segs_own
        for g in range(4):
            wdw = POOL_W[g]
            for oc in range(4):
                fc = 4 * g + oc
                if oc % 2 == 0:
                    wt, rwt = wload(c, w_in[:, :, fc * 128: fc * 128 + 256], lambda t: t.rearrange("p (k c) -> p k c", k=8))
                ub = U[fc % 2]
                rub = RU[fc % 2]

                def evac_u(pst, rps, c0, n, tids, ub=ub, rub=rub):
                    if c0 == 0:
                        P.add("act", lambda e: e.activation(out=ub[:, 16:16 + n], in_=pst[:, 0:n], func=AF.Copy, scale=hflag), reads=[rps, Rcst], writes=[rub])
                    else:
                        P.add("act", lambda e: e.copy(out=ub[:, 16 + c0:16 + c0 + n], in_=pst[:, 0:n]), reads=[rps], writes=[rub])

                linear_fm(wt, rwt, (oc % 2) * 128, 8, lambda k, c0, n: hT[:, k, c0:c0 + n], lambda c0, n, tids: [RhT[t] for t in tids], segs, evac_u)
                pool_segs = [(16, TH + 512)] + [(16 + TH + 512 * s, 512) for s in range(1, T // 512)] if not last else [(16 + TH + 512 * s, 512) for s in range(T // 512)]
                for (p0, n) in pool_segs:
                    W_ = n + 16
                    us = ub[:, p0 - 16:p0 + n]
                    stages = []
                    srcs = [(us, rub)]
                    bufs = [(A_, RA), (B_, RB)]
                    sh = 1
                    nst = g + 1
                    for si in range(nst):
                        (src, rsrc) = srcs[-1]
                        (dst, rdst) = bufs[si % 2]
                        lo = 2 * sh - 1
                        P.add("dve", lambda e, src=src, dst=dst, lo=lo, sh=sh, W_=W_: e.tensor_tensor(out=dst[:, lo:W_], in0=src[:, lo:W_], in1=src[:, lo - sh:W_ - sh], op=ALU.add),
                              reads=[rsrc], writes=[rdst])
                        srcs.append((dst[:, 0:W_], rdst))
                        sh *= 2
                    (S_, rS) = srcs[-1]
                    o0 = p0 - 16
                    P.add("dve", lambda e, S_=S_, us=us, W_=W_, o0=o0, n=n, oc=oc, wdw=wdw: e.scalar_tensor_tensor(out=PTb[:, oc, o0:o0 + n], in0=S_[:, 16:W_], scalar=1.0 / wdw, in1=us[:, 16:W_], op0=ALU.mult, op1=ALU.subtract),
                          reads=[rS, rub], writes=[RPTb[oc]])
                    if p0 <= 16 + TH < p0 + n:
                        q = 16 + TH - (p0 - 16)
                        P.add("dve", lambda e, S_=S_, q=q, g=g: e.tensor_tensor(out=tmp16[:], in0=S_[:, q:q + 16], in1=cst[:, 16 * g:16 * g + 16], op=ALU.mult), reads=[rS, Rcst], writes=[Rt16])
                        P.add("dve", lambda e, us=us, q=q, oc=oc: e.tensor_tensor(out=PTb[:, oc, TH:TH + 16], in0=tmp16[:], in1=us[:, q:q + 16], op=ALU.subtract), reads=[Rt16, rub], writes=[RPTb[oc]])
            gstate = {}

            def do_gate(oc, g=g, l=l):
                fc = 4 * g + oc
                if oc % 2 == 0:
                    gstate["wtg"] = wload(c, w_in[:, :, DA + fc * 128: DA + fc * 128 + 256], lambda t: t.rearrange("p (k c) -> p k c", k=8))
                wtg, rwtg = gstate["wtg"]
                sgb, rsgb = SG[oc % 3], RSG[oc % 3]

                def evac_gate(pst, rps, c0, n, tids, sgb=sgb, rsgb=rsgb):
                    P.add("act", lambda e: e.activation(out=sgb[:, c0:c0 + n], in_=pst[:, 0:n], func=AF.Silu), reads=[rps], writes=[rsgb])

                linear_fm(wtg, rwtg, (oc % 2) * 128, 8, lambda k, c0, n: hT[:, k, c0:c0 + n], lambda c0, n, tids: [RhT[t] for t in tids], segs_g, evac_gate)

            def do_grp(oc, g=g, l=l):
                fc = 4 * g + oc
                if "wgrp" not in gstate:
                    gstate["wgrp"] = wload(c, io["w_grp_a"][l, g].rearrange("(k p) c -> p k c", p=128), lambda t: t.rearrange("p (k c) -> p k c", k=4))
                wgrp, rwgrp = gstate["wgrp"]
                sgb, rsgb = SG[oc % 3], RSG[oc % 3]

                def evac_mix(pst, rps, c0, n, tids, sgb=sgb, rsgb=rsgb, oc=oc, fc=fc, l=l):
                    P.add("dve", lambda e: e.scalar_tensor_tensor(out=GT[:, oc, c0:c0 + n], in0=pst[:, 0:n], scalar=scl[:, l, fc:fc + 1], in1=sgb[:, c0:c0 + n], op0=ALU.mult, op1=ALU.mult),
                          reads=[rps, rsgb, Rscl], writes=[RGT[oc]])

                linear_fm(wgrp, rwgrp, oc * 128, 4, lambda k, c0, n: PTb[:, k, c0:c0 + n], lambda c0, n, tids: list(RPTb), segs_g, evac_mix)

            do_gate(0)
            do_gate(1)
            do_grp(0)
            do_gate(2)
            do_grp(1)
            do_gate(3)
            do_grp(2)
            do_grp(3)
            w_out_g = io["w_out_a"][l, 512 * g:512 * (g + 1), :].rearrange("(k p) c -> p k c", p=128)
            tl = tiles if not last else tiles[1:]
            for hf in range(2):
                wo, rwo = wload(c, w_out_g[:, :, 512 * hf:512 * (hf + 1)], lambda t: t.rearrange("p (k c) -> p k c", k=4))
                for (ti, rows, col0) in tl:
                    pst, rps = psum_next(c)
                    for k in range(4):
                        P.add("pe", lambda e, k=k, pst=pst, rows=rows, col0=col0, wo=wo: e.matmul(pst[0:rows, :], lhsT=GT[:, k, col0:col0 + rows], rhs=wo[:, k, :], start=(k == 0), stop=(k == 3)),
                              reads=[rwo] + RGT, writes=[rps])
                    hs = H[0:rows, ti, 512 * hf:512 * (hf + 1)]
                    if g == 0:
                        P.add("dve", lambda e, hs=hs, pst=pst, rows=rows: e.scalar_tensor_tensor(out=hs, in0=hs, scalar=ALPHA, in1=pst[0:rows, :], op0=ALU.mult, op1=ALU.add), reads=[rps, RH[ti]], writes=[RH[ti]])
                    else:
                        P.add("dve", lambda e, hs=hs, pst=pst, rows=rows: e.tensor_tensor(out=hs, in0=hs, in1=pst[0:rows, :], op=ALU.add), reads=[rps, RH[ti]], writes=[RH[ti]])
                    if g == 3 and hf == 1:
                        lnp.push(H[:, ti, :], RH[ti], rows, col0=col0, rhT=RhT[ti])
            if g == 3:
                lnp.flush()

    w_kv = io["w_kv"].rearrange("(k p) c -> p k c", p=128)
    outs = []
    segs_own = segs[1:]
    tail = io.get("tail")
    pend_cc = []
    for g in (2, 1, 0):
        d = DIL[g]
        nb = T // (128 * d)
        rko, rvo = [], []
        for cc in range(8):
            if cc % 2 == 0:
                wt, rwt = wload(c, w_kv[:, :, g * 1024 + cc * 128: g * 1024 + cc * 128 + 256], lambda t: t.rearrange("p (k c) -> p k c", k=8))
            stg = PTb[:, cc % 2, 0:T]
            rstg = RPTb[cc % 2]

            def evac_k(pst, rps, c0, n, tids, stg=stg, rstg=rstg, d=d, nb=nb):
                s = (c0 - TH) // 512
                if d == 1:
                    P.add("act", lambda e: e.copy(out=stg[:, 512 * s:512 * (s + 1)], in_=pst[:, :]), reads=[rps], writes=[rstg])
                elif d == 4:
                    o = stg.rearrange("p (r n i) -> p r n i", r=4, n=nb)[:, :, s, :]
                    P.add("act", lambda e: e.copy(out=o, in_=pst[:, :].rearrange("p (i r) -> p r i", r=4)), reads=[rps], writes=[rstg])
                else:
                    o = stg.rearrange("p (r i) -> p r i", r=16)[:, :, 32 * s:32 * (s + 1)]
                    P.add("act", lambda e: e.copy(out=o, in_=pst[:, :].rearrange("p (i r) -> p r i", r=16)), reads=[rps], writes=[rstg])

            linear_fm(wt, rwt, (cc % 2) * 128, 8, lambda k, c0, n: hT[:, k, c0:c0 + n], lambda c0, n, tids: [RhT[t] for t in tids], segs_own, evac_k)
            rk = Res("ktd")
            rko.append(rk)
            outs.append(P.dma("sp", io["KT"][g, cc * 128:(cc + 1) * 128, :], stg, reads=[rstg], writes=[rk], key="kt%d" % (cc % 2)))
            if tail is not None:
                frows = slice(cc * 128, (cc + 1) * 128)
                parts = [(0, d)] if g < 2 else [(0, 8), (8, 16)]
                for (ra, rb) in parts:
                    cv, ck, off = tail_loc(g, ra)
                    n_ = 128 * (rb - ra)
                    r2 = Res("tlk")
                    src = stg.rearrange("p (r n i) -> p r n i", r=d, n=nb)[:, ra:rb, nb - 1, :]
                    P.dma("sp", tail["TL"][ck][frows, off:off + n_].rearrange("p (r i) -> p r i", i=128), src, reads=[rstg], writes=[r2], key="tk%d_%d" % (ck, cc % 2))
                    tail["rtl"][ck].append(r2)
        for vc in range(4):
            wt, rwt = wload(c, w_kv[:, :, 3072 + g * 1024 + vc * 256: 3072 + g * 1024 + (vc + 1) * 256], lambda t: t.rearrange("p (k c) -> p k c", k=8))
            vi = (g * 4 + vc) % 2
            stg = U[vi][:].bitcast(BF16)[:, 0:4096].rearrange("p (t c) -> p t c", t=NTT)
            rstg = RU[vi]
            for (ti, rows, col0) in tiles[1:]:
                pst, rps = psum_next(c)
                for k in range(8):
                    P.add("pe", lambda e, k=k, pst=pst, col0=col0, wt=wt: e.matmul(pst[:, 0:256], lhsT=hT[:, k, col0:col0 + 128], rhs=wt[:, k, :], start=(k == 0), stop=(k == 7)),
                          reads=[rwt, RhT[ti]], writes=[rps])
                P.add("dve", lambda e, pst=pst, ti=ti, stg=stg: e.tensor_copy(out=stg[:, ti - 1, :], in_=pst[:, 0:256]), reads=[rps], writes=[rstg])
            rv = Res("vd")
            rvo.append(rv)
            outs.append(P.dma("sp", io["V"][g].rearrange("(t p) c -> p t c", p=128)[:, :, vc * 256:(vc + 1) * 256], stg, reads=[rstg], writes=[rv], key="v%d" % vi))
        if tail is not None:
            TL, tl32, ga32, rtl, rga = tail["TL"], tail["tl32"], tail["ga32"], tail["rtl"], tail["rga"]
            vsrc = io["V"][g, T - 128 * d:T, :].rearrange("(i r) f -> r i f", r=d)
            ksrc = io["KT"][g].rearrange("p (r n i) -> p r n i", r=d, n=nb)[:, :, nb - 1, :]
            parts = [(0, d)] if g < 2 else [(0, 8), (8, 16)]
            for (ra, rb) in parts:
                cv, ck, off = tail_loc(g, ra)
                n_ = 128 * (rb - ra)
                r1 = Res("tlv")
                P.dma("sp", TL[cv][off:off + n_, :].rearrange("(r i) f -> r i f", i=128), vsrc[ra:rb], reads=rvo, writes=[r1], key="tlv%d%d" % (g, ra))
                rtl[cv].append(r1)
            for f in pend_cc:
                f()
            pend_cc = []
            chunks = {2: (4, 1, 5, 2), 1: (3, 0), 0: (7, 6)}[g]
            for ci in chunks:
                def issue(ci=ci):
                    P.add("pool", lambda e: e.collective_compute("AllGather", ALU.bypass, replica_groups=[[0, 1], [2, 3], [4, 5], [6, 7]], ins=[tl32[ci]], outs=[ga32[ci]]),
                          reads=rtl[ci], writes=[rga[ci]], is_dma=True, key="cc%d" % ci, inc=1)
                pend_cc.append(issue)
    for f in pend_cc:
        f()
    if "H1" in io:
        for i in range(NTT):
            outs.append(P.dma("sp", io["H1"][128 * i:128 * (i + 1), :], H[:, 1 + i, :], reads=[RH[1 + i]], key="ho"))
    return outs


def make_nc_a():
    nc = bass.Bass("TRN2", target_bir_lowering=False)
    io = {}
    io["xh"] = nc.dram_tensor("xh", [TX, D], F32, kind="ExternalInput").ap()
    io["w_in_a"] = nc.dram_tensor("w_in_a", [2, D, 2 * DA], F32, kind="ExternalInput").ap()
    io["w_grp_a"] = nc.dram_tensor("w_grp_a", [2, 4, 512, 512], F32, kind="ExternalInput").ap()
    io["scale_a"] = nc.dram_tensor("scale_a", [2, DA], F32, kind="ExternalInput").ap()
    io["w_out_a"] = nc.dram_tensor("w_out_a", [2, DA, D], F32, kind="ExternalInput").ap()
    io["w_kv"] = nc.dram_tensor("w_kv", [D, 6144], F32, kind="ExternalInput").ap()
    io["ln_g"] = nc.dram_tensor("ln_g", [4, D], F32, kind="ExternalInput").ap()
    io["ln_b"] = nc.dram_tensor("ln_b", [4, D], F32, kind="ExternalInput").ap()
    io["cst"] = nc.dram_tensor("cst", [128, 80], F32, kind="ExternalInput").ap()
    io["ident"] = nc.dram_tensor("ident", [128, 128], F32, kind="ExternalInput").ap()
    io["KT"] = nc.dram_tensor("KT", [3, D, T], BF16, kind="ExternalOutput").ap()
    io["V"] = nc.dram_tensor("V", [3, T, D], BF16, kind="ExternalOutput").ap()
    io["H1"] = nc.dram_tensor("H1", [T, D], F32, kind="ExternalOutput").ap()
    with contextlib.ExitStack() as st:
        P = Prog(nc)
        c = _mk_common(nc, P, st)
        outs = build_phase_a(nc, c, io)
        P.finish("sp", outs)
        P.emit()
    return nc


def core_consts(hf):
    cst = np.zeros((128, 80), np.float32)
    for g, w in enumerate(POOL_W):
        for t in range(16):
            cnt = min(t + 1, w) if hf == 0 else w
            cst[:, 16 * g + t] = 1.0 / cnt
    cst[:, 64] = 0.0 if hf == 0 else 1.0
    cst[:, 65] = EPS
    return cst


def run_phase_a(inputs):
    x = np.asarray(inputs["x"], np.float32)
    in_maps = []
    for core in range(8):
        b, hf = core // 2, core % 2
        xh = np.zeros((TX, D), np.float32)
        xh[TH:] = x[b, hf * T:(hf + 1) * T]
        if hf == 1:
            xh[:TH] = x[b, T - TH:T]
        m = {"xh": xh, "cst": core_consts(hf), "ident": np.eye(128, dtype=np.float32)}
        for k in ("w_in_a", "w_grp_a", "scale_a", "w_out_a", "w_kv", "ln_g", "ln_b"):
            m[k] = np.ascontiguousarray(np.asarray(inputs[k], np.float32))
        in_maps.append(m)
    nc = make_nc_a()
    res = run_bass_kernel_spmd(nc, in_maps, core_ids=list(range(8)))
    return res.results


def sl(start, count, step):
    return slice(start, start + (count - 1) * step + 1, step)


def unit_blocks(g, u):
    if g == 0:
        return 1, 8, 0, 8 * u
    if g == 1:
        return 2, 4, 2 * u, 0
    return 8, 1, 8 * u, 0


def build_phase_b(nc, c, io, hin, houts):
    P = c.P
    sb = c.sb
    hT = c.hT[:, :, TH:TX]
    GTb = sb("bGT", [128, 8, T], BF16)
    QT = [sb("bQT%d" % i, [128, 2, T], BF16) for i in range(3)]
    SGb = sb("bSG", [128, T], BF16)
    ACC = sb("bACC", [128, 2, T], F32)
    KB = [sb("bKB%d" % i, [128, 2048], BF16) for i in range(2)]
    VB = [sb("bVB%d" % i, [128, 2048], BF16) for i in range(2)]
    NPT, LAG = 4, 2
    PTs = [sb("bPT%d" % i, [128, 512], BF16) for i in range(NPT)]
    pend2 = []
    blk_i = [0]
    NHS = 4
    Hs = [sb("bHs%d" % i, [128, 1024], F32) for i in range(NHS)]
    Gb = sb("bGb", [128, 1024], F32)
    Bb = sb("bBb", [128, 1024], F32)
    ident = sb("bident", [128, 128], BF16)
    ETb = [sb("bET%d" % i, [128, 2, 512], F32) for i in range(2)]
    ones64 = sb("bones", [128, 64], BF16)

    RhT = c.RhT[1:]
    RGT = [Res("bGT%d" % i) for i in range(8)]
    RQT = [Res("bQT%d" % i) for i in range(3)]
    RSG, RACC = Res("bSG"), Res("bACC")
    RKBo = [Res("kbo0"), Res("kbo1")]
    RKBp = [Res("kbp0"), Res("kbp1")]
    RVBo = [[Res("vbo0a"), Res("vbo0b")], [Res("vbo1a"), Res("vbo1b")]]
    RVBp = [Res("vbp0"), Res("vbp1")]
    RPTs = [Res("pts%d" % i) for i in range(6)]
    RHs = [Res("hs%d" % i) for i in range(4)]
    RGB = Res("bGB")
    Rhb = [Res("bhb0"), Res("bhb1")]
    Rsm = [Res("bsm0"), Res("bsm1")]
    Rid, Rmask, RKA, Rones = Res("bid"), Res("bmask"), Res("bKA"), Res("bones")
    RQA = [Res("qa0"), Res("qa1")]

    P.dma("pool", ident[:], io["ident"], writes=[Rid], key="bc0")
    RET = [Res("et0"), Res("et1")]
    et_state = {"k": 0, "key": [None, None]}
    P.add("dve", lambda e: e.memset(ones64[:], 1.0), writes=[Rones])
    for g in range(3):
        P.add("dve", lambda e, g=g: e.memset(QT[g][64:128, 0, :], 0.0), writes=[RQT[g]])
        P.add("dve", lambda e, g=g: e.memset(QT[g][0:64, 1, :], 0.0), writes=[RQT[g]])

    tiles = [(i, 128, 128 * i) for i in range(NTT)]
    segs = [(512 * s_, 512, [4 * s_ + q for q in range(4)]) for s_ in range(T // 512)]

    kvi = [0]
    lnp = LNPipe(c, "lnB", Gb, Bb, RGB, hT, ident, Rid)

    for l in range(2):
        P.dma("sp", Gb[:], io["ln_g"][2 + l:3 + l, :].partition_broadcast(128).rearrange("p o d -> p (o d)"), writes=[RGB], key="bgb")
        P.dma("sp", Bb[:], io["ln_b"][2 + l:3 + l, :].partition_broadcast(128).rearrange("p o d -> p (o d)"), writes=[RGB], key="bgb")
        w_in = io["w_in_b"][l].rearrange("(k p) (g c) -> p k g c", p=128, g=4)

        def load_j(jj, w_in=w_in):
            return (wload2(c, [w_in[:, :, 0, 128 * jj:128 * (jj + 1)], w_in[:, :, 1, 128 * jj:128 * (jj + 1)]]),
                    wload2(c, [w_in[:, :, 2, 128 * jj:128 * (jj + 1)], w_in[:, :, 3, 128 * jj:128 * (jj + 1)]]))

        for j in range(8):
            if j == 0:
                nxt = load_j(0)
            (wA, rwA), (wB, rwB) = nxt
            if j < 7:
                nxt = load_j(j + 1)
            else:
                w_out_pre = io["w_out_b"][l].rearrange("(k p) c -> p k c", p=128)
                wos_pre = [wload(c, w_out_pre[:, :, 256 * q:256 * (q + 1)], lambda t: t.rearrange("p (k c) -> p k c", k=8)) for q in range(2)]
            for g in range(4):
                wt, rwt, gi = (wA, rwA, g) if g < 2 else (wB, rwB, g - 2)
                for (c0, n, tids) in segs:
                    pst, rps = psum_next(c)
                    for k in range(8):
                        P.add("pe", lambda e, k=k, c0=c0, pst=pst, wt=wt, gi=gi: e.matmul(pst[:, :], lhsT=wt[:, k, gi, :], rhs=hT[:, k, c0:c0 + 512], start=(k == 0), stop=(k == 7)),
                              reads=rwt + [RhT[t] for t in tids], writes=[rps])
                    s_ = c0 // 512
                    if g == 3:
                        P.add("act", lambda e, pst=pst, c0=c0: e.activation(out=SGb[:, c0:c0 + 512], in_=pst[:, :], func=AF.Silu), reads=[rps], writes=[RSG])
                    else:
                        d = DIL[g]
                        nb = T // (128 * d)
                        for hh in range(2):
                            hp = slice(64 * hh, 64 * hh + 64)
                            if d == 1:
                                o, i_ = QT[g][hp, hh, c0:c0 + 512], pst[hp, :]
                            elif d == 4:
                                o = QT[g][hp, hh, :].rearrange("p (r n i) -> p r n i", r=4, n=nb)[:, :, s_, :]
                                i_ = pst[hp, :].rearrange("p (i r) -> p r i", r=4)
                            else:
                                o = QT[g][hp, hh, :].rearrange("p (r i) -> p r i", r=16)[:, :, 32 * s_:32 * (s_ + 1)]
                                i_ = pst[hp, :].rearrange("p (i r) -> p r i", r=16)
                            P.add("act", lambda e, o=o, i_=i_: e.activation(out=o, in_=i_, func=AF.Copy, scale=0.125), reads=[rps], writes=[RQT[g]])
            uorder = [(g_, u_) for g_ in range(3) for u_ in range(2)]
            acc_copy = True
            if l == 0 and j == 0:
                uorder = [(0, 1), (2, 0), (2, 1), (1, 0), (1, 1), (0, 0)]
                acc_copy = False
                P.add("dve", lambda e: e.memset(ACC[:, :, :], 0.0), writes=[RACC])
            for (g, u) in uorder:
                d = DIL[g]
                nb = T // (128 * d)
                if True:
                    R_, nbk, r0, n0 = unit_blocks(g, u)
                    kb = kvi[0] % 2
                    kvi[0] += 1
                    rows = slice(128 * j, 128 * (j + 1))
                    KBv = KB[kb][:, 0:R_ * (nbk + 1) * 128].rearrange("p (r b i) -> p r b i", r=R_, b=nbk + 1)
                    VBv = VB[kb][:, 0:R_ * (nbk + 1) * 128].rearrange("p (r b i) -> p r b i", r=R_, b=nbk + 1)
                    W_ = (nbk + 1) * 128
                    KBf = KB[kb][:, 0:R_ * W_].rearrange("p (r c) -> p r c", r=R_)
                    ksrc = io["KT"][g, rows, :].rearrange("p (r c) -> p r c", r=d)[:, r0:r0 + R_, n0 * 128:(n0 + nbk) * 128]
                    P.dma("sp", KBf[:, :, 128:W_], ksrc, writes=[RKBo[kb]], key="kbo%d" % kb)
                    vall = io["V"][g].rearrange("(n i dd) f -> i dd n f", i=128, dd=d)
                    if nbk == 1:
                        P.dma("sp", VBv[:, :, 1, :], vall[:, r0:r0 + R_, n0, rows], writes=[RVBo[kb][0]], key="vbo%d_0" % kb)
                    else:
                        for ri in range(R_):
                            P.dma("sp", VBv[:, ri, 1:1 + nbk, :], vall[:, r0 + ri, n0:n0 + nbk, rows], writes=[RVBo[kb][ri]], key="vbo%d_%d" % (kb, ri))
                    if n0 > 0:
                        P.dma("sp", KBv[:, 0, 0, :], io["KT"][g, rows, (n0 - 1) * 128:n0 * 128], writes=[RKBp[kb]], key="kbp%d" % kb)
                        P.dma("sp", VBv[:, 0, 0, :], io["V"][g, (n0 - 1) * 128:n0 * 128, rows], writes=[RVBp[kb]], key="vbp%d" % kb)
                    else:
                        kps = io["KTp_fn"](g, r0, R_)[rows, :].rearrange("p (r i) -> p r i", i=128)
                        vps = io["Vp_fn"](g, r0, R_)[:, rows].rearrange("(r i) f -> i r f", i=128)
                        P.dma("sp", KBv[:, :, 0, :], kps, reads=io["prev_res"](g, r0, 1), writes=[RKBp[kb]], key="kbp%d" % kb)
                        P.dma("sp", VBv[:, :, 0, :], vps, reads=io["prev_res"](g, r0, 0), writes=[RVBp[kb]], key="vbp%d" % kb)
                    kvres = [RKBo[kb], RKBp[kb]]
                    vvres = RVBo[kb] + [RVBp[kb]]
                    if (j, g) in et_state["key"]:
                        eb = et_state["key"].index((j, g))
                    else:
                        eb = et_state["k"] % 2
                        et_state["k"] += 1
                        et_state["key"][eb] = (j, g)
                        P.dma("sp", ETb[eb][:], io["ET"][j, g].rearrange("v c x -> c v x"), writes=[RET[eb]], key="et%d" % eb)
                    for ri in range(R_):
                        for bi in range(nbk):
                            r = r0 + ri
                            n = n0 + bi
                            qcol = (r * nb + n) * 128
                            pb = blk_i[0] % NPT
                            blk_i[0] += 1

                            def stage1(g=g, n=n, ri=ri, bi=bi, qcol=qcol, pb=pb, KBv=KBv, kvres=kvres, eb=eb):
                                pst, rps = psum_next(c)
                                for ch in range(2):
                                    oc_ = ch * 256
                                    P.add("pe", lambda e, oc_=oc_, ch=ch: e.matmul(pst[:, oc_:oc_ + 256], lhsT=KBv[:, ri, bi + ch, :], rhs=QT[g][:, :, qcol:qcol + 128], start=True, stop=True),
                                          reads=kvres + [RQT[g]], writes=[rps])
                                P.add("act", lambda e: e.activation(out=PTs[pb][:, :], in_=pst[:, :], func=AF.Exp), reads=[rps], writes=[RPTs[pb]])
                                var = 0 if n == 0 else 1
                                meng = "pool" if (pb % 2 == 0) else "dve"
                                P.add(meng, lambda e: e.tensor_tensor(out=PTs[pb][:, :], in0=PTs[pb][:, :], in1=ETb[eb][:, var, :], op=ALU.mult),
                                      reads=[RPTs[pb], RET[eb]], writes=[RPTs[pb]])

                            def stage2(g=g, n=n, r=r, d=d, ri=ri, bi=bi, pb=pb, VBv=VBv, vvres=vvres, acc_copy=acc_copy):
                                pnd, rpnd = psum_next(c)
                                for hh in range(2):
                                    hp = slice(64 * hh, 64 * hh + 64)
                                    for ch in range(2):
                                        oc_ = (ch * 2 + hh) * 128
                                        P.add("pe", lambda e, hp=hp, ch=ch, oc_=oc_: e.matmul(pnd[hp, 0:128], lhsT=VBv[:, ri, bi + ch, hp], rhs=PTs[pb][:, oc_:oc_ + 128], start=(ch == 0), stop=(ch == 1)),
                                              reads=vvres + [RPTs[pb]], writes=[rpnd])
                                    for ch in range(2):
                                        oc_ = (ch * 2 + hh) * 128
                                        P.add("pe", lambda e, hp=hp, oc_=oc_, ch=ch: e.matmul(pnd[hp, 128:256], lhsT=ones64[:, :], rhs=PTs[pb][:, oc_:oc_ + 128], start=(ch == 0), stop=(ch == 1)),
                                              reads=[Rones, RPTs[pb]], writes=[rpnd])
                                av = ACC[:, :, sl(128 * n * d + r, 128, d)]
                                pv = pnd[:, 0:256].rearrange("p (x i) -> p x i", x=2)
                                if g == 0 and acc_copy:
                                    P.add("dve", lambda e: e.tensor_copy(out=av, in_=pv), reads=[rpnd], writes=[RACC])
                                else:
                                    P.add("dve", lambda e: e.tensor_tensor(out=av, in0=av, in1=pv, op=ALU.add), reads=[rpnd, RACC], writes=[RACC])

                            stage1()
                            pend2.append(stage2)
                            if len(pend2) > LAG:
                                pend2.pop(0)()
            for f2 in pend2:
                f2()
            del pend2[:]
            P.add("act", lambda e: e.activation(out=ACC[:, 1, :], in_=ACC[:, 1, :], func=AF.Ln), reads=[RACC], writes=[RACC])
            P.add("act", lambda e: e.activation(out=ACC[:, 1, :], in_=ACC[:, 1, :], func=AF.Exp, scale=-1.0), reads=[RACC], writes=[RACC])
            P.add("dve", lambda e: e.tensor_tensor(out=ACC[:, 0, :], in0=ACC[:, 0, :], in1=ACC[:, 1, :], op=ALU.mult), reads=[RACC], writes=[RACC])
            P.add("dve", lambda e, j=j: e.tensor_tensor(out=GTb[:, j, :], in0=ACC[:, 0, :], in1=SGb[:, :], op=ALU.mult), reads=[RACC, RSG], writes=[RGT[j]])
        w_out = io["w_out_b"][l].rearrange("(k p) c -> p k c", p=128)
        wos = wos_pre + [wload(c, w_out[:, :, 256 * q:256 * (q + 1)], lambda t: t.rearrange("p (k c) -> p k c", k=8)) for q in range(2, 4)]
        hsrc = hin if l == 0 else houts[0]
        stores = []
        for (ti, rows_, col0) in tiles:
            b = ti % NHS
            P.dma("sp", Hs[b][:], hsrc[128 * ti:128 * (ti + 1), :], writes=[RHs[b]], key="hs%d" % b)
            for q in range(4):
                wo, rwo = wos[q]
                pst, rps = psum_next(c)
                for k in range(8):
                    P.add("pe", lambda e, k=k, pst=pst, col0=col0, wo=wo: e.matmul(pst[:, 0:256], lhsT=GTb[:, k, col0:col0 + 128], rhs=wo[:, k, :], start=(k == 0), stop=(k == 7)),
                          reads=[rwo, RGT[k]], writes=[rps])
                hs = Hs[b][:, 256 * q:256 * (q + 1)]
                P.add("dve", lambda e, hs=hs, pst=pst: e.scalar_tensor_tensor(out=hs, in0=hs, scalar=ALPHA, in1=pst[:, 0:256], op0=ALU.mult, op1=ALU.add), reads=[rps, RHs[b]], writes=[RHs[b]])

            def after(ti=ti, b=b, l=l):
                stores.append(P.dma("sp", houts[l][128 * ti:128 * (ti + 1), :], Hs[b][:], reads=[RHs[b]], key="ho%d" % b))

            lnp.push(Hs[b], RHs[b], 128, col0=col0, rhT=RhT[ti], after=after, tr=(l == 0))
        lnp.flush()
    return stores


def make_nc_b():
    nc = bass.Bass("TRN2", target_bir_lowering=False)
    io = {}
    io["H1"] = nc.dram_tensor("H1", [T, D], F32, kind="ExternalInput").ap()
    io["KT"] = nc.dram_tensor("KT", [3, D, T], BF16, kind="ExternalInput").ap()
    io["V"] = nc.dram_tensor("V", [3, T, D], BF16, kind="ExternalInput").ap()
    io["KTp"] = nc.dram_tensor("KTp", [3, D, 2048], BF16, kind="ExternalInput").ap()
    io["Vp"] = nc.dram_tensor("Vp", [3, 2048, D], BF16, kind="ExternalInput").ap()
    io["w_in_b"] = nc.dram_tensor("w_in_b", [2, D, 4096], F32, kind="ExternalInput").ap()
    io["w_out_b"] = nc.dram_tensor("w_out_b", [2, D, D], F32, kind="ExternalInput").ap()
    io["ln_g"] = nc.dram_tensor("ln_g", [4, D], F32, kind="ExternalInput").ap()
    io["ln_b"] = nc.dram_tensor("ln_b", [4, D], F32, kind="ExternalInput").ap()
    io["ident"] = nc.dram_tensor("ident", [128, 128], F32, kind="ExternalInput").ap()
    io["KA"] = nc.dram_tensor("KA", [7, 128], F32, kind="ExternalInput").ap()
    io["QA"] = nc.dram_tensor("QA", [8, 7, 9 * 2 * 128], F32, kind="ExternalInput").ap()
    h2 = nc.dram_tensor("H2", [T, D], F32, kind="Internal").ap()
    out = nc.dram_tensor("out", [T, D], F32, kind="ExternalOutput").ap()
    with contextlib.ExitStack() as st:
        P = Prog(nc)
        c = _mk_common(nc, P, st)
        stores = build_phase_b(nc, c, io, io["H1"], [h2, out])
        P.finish("sp", stores)
        P.emit()
    return nc


def _split3(x):
    x = np.asarray(x, np.float64)
    hi = x.astype(ml_dtypes.bfloat16).astype(np.float64)
    lo = (x - hi).astype(ml_dtypes.bfloat16).astype(np.float64)
    lo2 = (x - hi - lo).astype(ml_dtypes.bfloat16).astype(np.float64)
    return hi, lo, lo2


def attn_consts(hf):
    a = np.arange(128, dtype=np.int64)[None, :]
    cc = np.arange(128, dtype=np.int64)[:, None]
    i = np.arange(1, 49, dtype=np.float32)
    slopes = (np.float32(2.0) ** (np.float32(-8.0) * i / np.float32(48))).astype(np.float32).reshape(3, 16)
    ET = np.zeros((8, 3, 2, 128, 2, 2, 128), np.float32)
    rel_p = 128 + a - cc
    rel_c = a - cc
    val_p = rel_p <= 128
    val_c = rel_c >= 0
    for j in range(8):
        for g in range(3):
            for hh in range(2):
                sl_ = slopes[g, 2 * j + hh]
                bp = -(sl_ * (DIL[g] * rel_p).astype(np.float32)).astype(np.float32)
                bc = -(sl_ * (DIL[g] * rel_c).astype(np.float32)).astype(np.float32)
                ep = np.where(val_p, np.exp(bp.astype(np.float64)), 0.0).astype(np.float32)
                ec = np.where(val_c, np.exp(bc.astype(np.float64)), 0.0).astype(np.float32)
                for var in range(2):
                    ET[j, g, var, :, 0, hh, :] = 0.0 if (var == 0 and hf == 0) else ep
                    ET[j, g, var, :, 1, hh, :] = ec
    return ET.reshape(8, 3, 2, 128, 512)


def run_phase_b(inputs, resA):
    in_maps = []
    bf = ml_dtypes.bfloat16
    for core in range(8):
        b, hf = core // 2, core % 2
        r = resA[core]
        KTp = np.zeros((3, D, 2048), bf)
        Vp = np.zeros((3, 2048, D), bf)
        if hf == 1:
            pr = resA[core - 1]
            for g in range(3):
                d = DIL[g]
                nb = T // (128 * d)
                kt = np.asarray(pr["KT"][g]).reshape(D, d, nb, 128)
                KTp[g, :, 0:d * 128] = kt[:, :, nb - 1, :].reshape(D, d * 128)
                v = np.asarray(pr["V"][g]).reshape(nb, 128, d, D)
                Vp[g, 0:d * 128, :] = v[nb - 1].transpose(1, 0, 2).reshape(d * 128, D)
        masks, KA, QA = attn_consts(hf)
        m = {"H1": np.asarray(r["H1"]), "KT": np.asarray(r["KT"]), "V": np.asarray(r["V"]), "KTp": KTp, "Vp": Vp,
             "ident": np.eye(128, dtype=np.float32), "masks": masks, "KA": KA, "QA": QA}
        for k in ("w_in_b", "w_out_b", "ln_g", "ln_b"):
            m[k] = np.ascontiguousarray(np.asarray(inputs[k], np.float32))
        in_maps.append(m)
    nc = make_nc_b()
    res = run_bass_kernel_spmd(nc, in_maps, core_ids=list(range(8)))
    return res.results


NCH = 8
TL_SHAPE = {0: (512, 512), 3: (1024, 256), 1: (1024, 512), 2: (1024, 512), 4: (1024, 512), 5: (1024, 512), 6: (128, 512), 7: (1024, 64)}


def tail_loc(g, r0):
    if g == 0:
        return 6, 7, 0
    if g == 1:
        return 0, 3, 128 * r0
    return (1, 4, 128 * r0) if r0 < 8 else (2, 5, 128 * (r0 - 8))


def make_nc_fused():
    nc = bass.Bass("TRN2", target_bir_lowering=False)
    io = {}
    io["xh"] = nc.dram_tensor("xh", [TX, D], F32, kind="ExternalInput").ap()
    io["w_in_a"] = nc.dram_tensor("w_in_a", [2, D, 2 * DA], F32, kind="ExternalInput").ap()
    io["w_grp_a"] = nc.dram_tensor("w_grp_a", [2, 4, 512, 512], F32, kind="ExternalInput").ap()
    io["scale_a"] = nc.dram_tensor("scale_a", [2, DA], F32, kind="ExternalInput").ap()
    io["w_out_a"] = nc.dram_tensor("w_out_a", [2, DA, D], F32, kind="ExternalInput").ap()
    io["w_kv"] = nc.dram_tensor("w_kv", [D, 6144], F32, kind="ExternalInput").ap()
    io["w_in_b"] = nc.dram_tensor("w_in_b", [2, D, 4096], F32, kind="ExternalInput").ap()
    io["w_out_b"] = nc.dram_tensor("w_out_b", [2, D, D], F32, kind="ExternalInput").ap()
    io["ln_g"] = nc.dram_tensor("ln_g", [4, D], F32, kind="ExternalInput").ap()
    io["ln_b"] = nc.dram_tensor("ln_b", [4, D], F32, kind="ExternalInput").ap()
    io["cst"] = nc.dram_tensor("cst", [128, 80], F32, kind="ExternalInput").ap()
    io["ident"] = nc.dram_tensor("ident", [128, 128], F32, kind="ExternalInput").ap()
    io["ET"] = nc.dram_tensor("ET", [8, 3, 2, 128, 512], F32, kind="ExternalInput").ap()
    io["KT"] = nc.dram_tensor("KT", [3, D, T], BF16, kind="Internal").ap()
    io["V"] = nc.dram_tensor("V", [3, T, D], BF16, kind="Internal").ap()
    io["H1"] = nc.dram_tensor("H1", [T, D], F32, kind="Internal").ap()
    h2 = nc.dram_tensor("H2", [T, D], F32, kind="Internal").ap()
    tl32 = [nc.dram_tensor("TL%d" % i, list(TL_SHAPE[i]), F32, kind="Internal").ap() for i in range(NCH)]
    ga32 = [nc.dram_tensor("GA%d" % i, [2 * TL_SHAPE[i][0], TL_SHAPE[i][1]], F32, kind="Internal").ap() for i in range(NCH)]
    out = nc.dram_tensor("out", [T, D], F32, kind="ExternalOutput").ap()
    TL = [t.bitcast(BF16) for t in tl32]
    GA = [t.bitcast(BF16) for t in ga32]
    with contextlib.ExitStack() as st:
        P = Prog(nc)
        c = _mk_common(nc, P, st)
        rga = [Res("GA%d" % i) for i in range(NCH)]
        io["tail"] = {"TL": TL, "tl32": tl32, "ga32": ga32, "rtl": [[] for _ in range(NCH)], "rga": rga}
        with contextlib.ExitStack() as stA:
            c.cur = stA
            build_phase_a(nc, c, io)
        P.barrier(exclude=["cc%d" % i for i in range(NCH)])

        def vp_fn(g, r0, R_):
            cv, ck, off = tail_loc(g, r0)
            return GA[cv][off:off + 128 * R_, :]

        def ktp_fn(g, r0, R_):
            cv, ck, off = tail_loc(g, r0)
            return GA[ck][0:1024, off:off + 128 * R_]

        def prev_res(g, r0, is_k):
            cv, ck, off = tail_loc(g, r0)
            return [rga[ck if is_k else cv]]

        io["Vp_fn"], io["KTp_fn"], io["prev_res"] = vp_fn, ktp_fn, prev_res
        with contextlib.ExitStack() as stB:
            c.cur = stB
            stores = build_phase_b(nc, c, io, io["H1"], [h2, out])
        P.finish("sp", stores)
        P.emit()
    return nc


_NC_CACHE = {}


def kernel(**inputs):
    x = np.asarray(inputs["x"], np.float32)
    in_maps = []
    wkeys = ("w_in_a", "w_grp_a", "scale_a", "w_out_a", "w_kv", "w_in_b", "w_out_b", "ln_g", "ln_b")
    ws = {k: np.ascontiguousarray(np.asarray(inputs[k], np.float32)) for k in wkeys}
    ident = np.eye(128, dtype=np.float32)
    ets = [attn_consts(0), attn_consts(1)]
    for core in range(8):
        b, hf = core // 2, core % 2
        xh = np.zeros((TX, D), np.float32)
        xh[TH:] = x[b, hf * T:(hf + 1) * T]
        if hf == 1:
            xh[:TH] = x[b, T - TH:T]
        m = {"xh": xh, "cst": core_consts(hf), "ident": ident, "ET": ets[hf]}
        m.update(ws)
        in_maps.append(m)
    if "nc" not in _NC_CACHE:
        _NC_CACHE["nc"] = make_nc_fused()
    res = run_bass_kernel_spmd(_NC_CACHE["nc"], in_maps, core_ids=list(range(8)))
    out = np.zeros((4, 4096, D), np.float32)
    for core in range(8):
        b, hf = core // 2, core % 2
        out[b, hf * T:(hf + 1) * T] = np.asarray(res.results[core]["out"], np.float32)
    return out
```
